# Optimizing a Trainium2 kernel written in Bass

```python
import math
import jax
import jax.numpy as jnp
from jax import lax
import numpy as np

D_MODEL = 4096
BATCH = 16
SEQ = 256
DEPTH = 2
DEC_BATCH = 4
DEC_SEQ = 2048
PAST_LEN = 256

GRID_W = 64
N_MIX_GROUPS = 4
MIX_W = D_MODEL
GROUP_W = MIX_W // N_MIX_GROUPS

HEAD_DIM = 128
ATT_HEADS = GROUP_W // HEAD_DIM
ATT_KV_HEADS = 2
ROPE_THETA = 10000.0
Q_BLOCK = 128

SSD_HEADDIM = 64
SSD_HEADS = GROUP_W // SSD_HEADDIM
SSD_STATE = 128
SSD_GROUPS = 2
SSD_CHUNK = 64
CONV_W = 4

GLA_HEADS = 4
GLA_DK = GROUP_W // 2 // GLA_HEADS
GLA_DV = GROUP_W // GLA_HEADS
GLA_RANK = 16
GLA_NORMALIZER = 16.0
GLA_CHUNK = 64

LRU_W = GROUP_W
LRU_BLOCKS = 8
LRU_BW = LRU_W // LRU_BLOCKS
LRU_C = 8.0

D_FF = 11008
N_MOD = 9
LN_EPS = 1e-5
RMS_EPS = 1e-6

ATT_Q = ATT_HEADS * HEAD_DIM
ATT_KV = ATT_KV_HEADS * HEAD_DIM
SSD_BC = SSD_GROUPS * SSD_STATE
SSD_CONV_CH = GROUP_W + 2 * SSD_BC
GLA_K = GLA_HEADS * GLA_DK
GLA_V = GLA_HEADS * GLA_DV
IN_SPLITS = (ATT_Q, ATT_KV, ATT_KV,
             GROUP_W, GROUP_W, SSD_BC, SSD_BC, 2 * SSD_HEADS,
             GLA_K, GLA_K, GLA_V, 2 * GLA_RANK, GLA_V,
             LRU_W, LRU_W)
IN_W = sum(IN_SPLITS)
IN_OFFSETS = tuple(int(o) for o in np.cumsum(IN_SPLITS)[:-1])

kernel_name = 'hybrid_flow_trunk_ctx_and_denoise'


def layer_norm(x, g, b):
    xf = x.astype(jnp.float32)
    mu = jnp.mean(xf, axis=-1, keepdims=True)
    var = jnp.mean(jnp.square(xf - mu), axis=-1, keepdims=True)
    return ((xf - mu) * lax.rsqrt(var + LN_EPS) * g + b).astype(x.dtype)


def rms_norm(x, g):
    xf = x.astype(jnp.float32)
    return (xf * lax.rsqrt(jnp.mean(xf * xf, axis=-1, keepdims=True) + RMS_EPS) * g).astype(x.dtype)


def rev(t):
    return jnp.flip(t, axis=1)


def dwconv(x, w, b):
    n_tok = x.shape[1]
    left = CONV_W // 2
    xp = jnp.pad(x, ((0, 0), (left, CONV_W - 1 - left), (0, 0)))
    return sum(xp[:, j:j + n_tok] * w[j] for j in range(CONV_W)) + b


def swiglu(x, w_gate, w_up, w_down):
    return (jax.nn.silu(x @ w_gate) * (x @ w_up)) @ w_down


def axial_rope(n_tok):
    rows = n_tok // GRID_W
    row = jnp.repeat(jnp.arange(rows, dtype=jnp.float32), GRID_W)
    col = jnp.tile(jnp.arange(GRID_W, dtype=jnp.float32), rows)
    n_freq = HEAD_DIM // 4
    inv = ROPE_THETA ** (-jnp.arange(n_freq, dtype=jnp.float32) / n_freq)
    ang = jnp.stack([row[:, None] * inv, col[:, None] * inv], axis=1)
    return jnp.cos(ang), jnp.sin(ang)


def apply_axial_rope(x, cos, sin):
    xf = x.astype(jnp.float32).reshape(*x.shape[:-1], 2, 2, HEAD_DIM // 4)
    x1, x2 = xf[..., 0, :], xf[..., 1, :]
    cs, sn = cos[None, :, None], sin[None, :, None]
    out = jnp.stack([x1 * cs - x2 * sn, x2 * cs + x1 * sn], axis=-2)
    return out.reshape(x.shape).astype(x.dtype)


def block_attention(q, k, v):
    bsz, n_q = q.shape[:2]
    rep = ATT_HEADS // ATT_KV_HEADS
    qb = q.reshape(bsz, n_q // Q_BLOCK, Q_BLOCK, ATT_KV_HEADS, rep, HEAD_DIM)
    qb = jnp.moveaxis(qb, 1, 0)
    scale = HEAD_DIM ** -0.5

    def one_block(q_blk):
        s = jnp.einsum('bqgrd,bkgd->bgrqk', q_blk, k).astype(jnp.float32) * scale
        p = jax.nn.softmax(s, axis=-1).astype(v.dtype)
        return jnp.einsum('bgrqk,bkgd->bqgrd', p, v)

    o = lax.map(one_block, qb)
    return jnp.moveaxis(o, 0, 1).reshape(bsz, n_q, ATT_HEADS * HEAD_DIM)


def ssd_chunk_scan(x, dt, a, bm, cm, h0):
    bsz, n_tok = x.shape[:2]
    n_chunk = n_tok // SSD_CHUNK
    rep = SSD_HEADS // SSD_GROUPS
    bh = jnp.repeat(bm, rep, axis=2)
    ch = jnp.repeat(cm, rep, axis=2)
    xdt = x * dt[..., None].astype(x.dtype)
    la = dt * a
    mask = jnp.tril(jnp.ones((SSD_CHUNK, SSD_CHUNK), bool))[None, :, :, None]

    def chunks(t):
        return jnp.moveaxis(t.reshape(bsz, n_chunk, SSD_CHUNK, *t.shape[2:]), 1, 0)

    def step(h, inp):
        xdt_c, la_c, b_c, c_c = inp
        cum = jnp.cumsum(la_c, axis=1)
        seg = jnp.exp(jnp.where(mask, cum[:, :, None] - cum[:, None], -jnp.inf)).astype(x.dtype)
        scores = jnp.einsum('bihn,bjhn->bijh', c_c, b_c) * seg
        y = (jnp.einsum('bijh,bjhp->bihp', scores, xdt_c)
             + jnp.einsum('bihn,bhpn->bihp', c_c, h) * jnp.exp(cum)[..., None].astype(x.dtype))
        to_end = jnp.exp(cum[:, -1:] - cum)[..., None].astype(x.dtype)
        h_new = (h * jnp.exp(cum[:, -1])[:, :, None, None].astype(h.dtype)
                 + jnp.einsum('bjhn,bjhp->bhpn', b_c * to_end, xdt_c))
        return h_new.astype(h.dtype), y

    h_last, ys = lax.scan(step, h0, (chunks(xdt), chunks(la), chunks(bh), chunks(ch)))
    return jnp.moveaxis(ys, 0, 1).reshape(x.shape), h_last


def gla_chunk_scan(q, k, v, g, s0):
    bsz, n_tok = q.shape[:2]
    n_chunk = n_tok // GLA_CHUNK
    mask = jnp.tril(jnp.ones((GLA_CHUNK, GLA_CHUNK), bool))[None, :, :, None, None]

    def chunks(t):
        return jnp.moveaxis(t.reshape(bsz, n_chunk, GLA_CHUNK, *t.shape[2:]), 1, 0)

    def step(s, inp):
        q_c, k_c, v_c, g_c = inp
        b = jnp.cumsum(g_c, axis=1)
        dec = jnp.exp(jnp.where(mask, b[:, :, None] - b[:, None], -jnp.inf)).astype(q_c.dtype)
        att = jnp.einsum('bihk,bjhk,bijhk->bijh', q_c, k_c, dec)
        o = (jnp.einsum('bijh,bjhv->bihv', att, v_c)
             + jnp.einsum('bihk,bhkv->bihv', q_c * jnp.exp(b).astype(q_c.dtype), s))
        b_end = b[:, -1]
        s_new = (s * jnp.exp(b_end)[..., None].astype(s.dtype)
                 + jnp.einsum('bjhk,bjhv->bhkv', k_c * jnp.exp(b_end[:, None] - b).astype(k_c.dtype), v_c))
        return s_new.astype(s.dtype), o

    s_last, os_ = lax.scan(step, s0, (chunks(q), chunks(k), chunks(v), chunks(g)))
    return jnp.moveaxis(os_, 0, 1).reshape(bsz, n_tok, v.shape[2], v.shape[3]), s_last


def rg_lru(x, wa, ba, wx, bx, lam, h0):
    bsz, n_tok, ch = x.shape
    xb = x.reshape(bsz, n_tok, LRU_BLOCKS, LRU_BW)
    r = jax.nn.sigmoid((jnp.einsum('btnc,ncd->btnd', xb, wa).reshape(bsz, n_tok, ch) + ba).astype(jnp.float32))
    i = jax.nn.sigmoid((jnp.einsum('btnc,ncd->btnd', xb, wx).reshape(bsz, n_tok, ch) + bx).astype(jnp.float32))
    log_a = -LRU_C * r * jax.nn.softplus(-lam.astype(jnp.float32))
    a = jnp.exp(log_a)
    u = jnp.sqrt(-jnp.expm1(2.0 * log_a)) * i * x.astype(jnp.float32)
    u = u.at[:, 0].add(a[:, 0] * h0.astype(jnp.float32))

    def combine(left, right):
        return left[0] * right[0], right[0] * left[1] + right[1]

    _, h = lax.associative_scan(combine, (a, u), axis=1)
    return h.astype(x.dtype), h[:, -1].astype(h0.dtype)


def mixer(h, ctx, lp):
    bsz, n_tok, _ = h.shape
    (aq, ak, av, sx, sz, sb, sc, sdt, gq, gk, gv, glr, gg, lx, lg) = jnp.split(h @ lp['w_in'], IN_OFFSETS, axis=-1)

    q = rms_norm(aq.reshape(bsz, n_tok, ATT_HEADS, HEAD_DIM), lp['q_norm'])
    k = rms_norm(ak.reshape(bsz, n_tok, ATT_KV_HEADS, HEAD_DIM), lp['k_norm'])
    v = av.reshape(bsz, n_tok, ATT_KV_HEADS, HEAD_DIM)
    if ctx is None:
        o_att = block_attention(q, k, v)
        ssd0 = jnp.zeros((bsz, 2, SSD_HEADS, SSD_HEADDIM, SSD_STATE), h.dtype)
        gla0 = jnp.zeros((bsz, 2, GLA_HEADS, GLA_DK, GLA_DV), h.dtype)
        lru0 = jnp.zeros((bsz, 2, LRU_W), h.dtype)
    else:
        ctx_k, ctx_v, ssd0, gla0, lru0 = ctx
        cos, sin = axial_rope(n_tok)
        k_all = jnp.concatenate([apply_axial_rope(k, cos, sin), ctx_k], axis=1)
        v_all = jnp.concatenate([v, ctx_v], axis=1)
        o_att = block_attention(apply_axial_rope(q, cos, sin), k_all, v_all)

    xbc = jax.nn.silu(dwconv(jnp.concatenate([sx, sb, sc], axis=-1), lp['ssd_conv_w'], lp['ssd_conv_b']))
    xs = xbc[..., :GROUP_W].reshape(bsz, n_tok, SSD_HEADS, SSD_HEADDIM)
    bm = xbc[..., GROUP_W:GROUP_W + SSD_BC].reshape(bsz, n_tok, SSD_GROUPS, SSD_STATE)
    cm = xbc[..., GROUP_W + SSD_BC:].reshape(bsz, n_tok, SSD_GROUPS, SSD_STATE)
    dt = jax.nn.softplus(sdt.reshape(bsz, n_tok, 2, SSD_HEADS).astype(jnp.float32) + lp['ssd_dt_bias'])
    a = -jnp.exp(lp['ssd_a_log'].astype(jnp.float32))
    y_f, ssd_f = ssd_chunk_scan(xs, dt[:, :, 0], a[0], bm, cm, ssd0[:, 0])
    y_b, ssd_b = ssd_chunk_scan(rev(xs), rev(dt[:, :, 1]), a[1], rev(bm), rev(cm), ssd0[:, 1])
    y = y_f + rev(y_b) + xs * (lp['ssd_d'][0] + lp['ssd_d'][1])[:, None]
    o_ssd = rms_norm(y.reshape(bsz, n_tok, GROUP_W) * jax.nn.silu(sz), lp['ssd_norm_w'])

    gq = gq.reshape(bsz, n_tok, GLA_HEADS, GLA_DK) * (GLA_DK ** -0.5)
    gk = gk.reshape(bsz, n_tok, GLA_HEADS, GLA_DK)
    gv = gv.reshape(bsz, n_tok, GLA_HEADS, GLA_DV)
    gate = jnp.einsum('btdr,drk->btdk', glr.reshape(bsz, n_tok, 2, GLA_RANK), lp['gla_gate_w']) + lp['gla_gate_b']
    glog = (jax.nn.log_sigmoid(gate.astype(jnp.float32)) / GLA_NORMALIZER).reshape(bsz, n_tok, 2, GLA_HEADS, GLA_DK)
    o_f, gla_f = gla_chunk_scan(gq, gk, gv, glog[:, :, 0], gla0[:, 0])
    o_b, gla_b = gla_chunk_scan(rev(gq), rev(gk), rev(gv), rev(glog[:, :, 1]), gla0[:, 1])
    o_gla = rms_norm(o_f + rev(o_b), lp['gla_norm_w']).reshape(bsz, n_tok, GROUP_W) * jax.nn.silu(gg)

    xl = dwconv(lx, lp['lru_conv_w'], lp['lru_conv_b'])
    h_f, lru_f = rg_lru(xl, lp['lru_wa'][0], lp['lru_ba'][0], lp['lru_wx'][0], lp['lru_bx'][0], lp['lru_lam'][0], lru0[:, 0])
    h_b, lru_b = rg_lru(rev(xl), lp['lru_wa'][1], lp['lru_ba'][1], lp['lru_wx'][1], lp['lru_bx'][1], lp['lru_lam'][1], lru0[:, 1])
    o_lru = (h_f + rev(h_b)) * jax.nn.gelu(lg)

    out = jnp.concatenate([o_att, o_ssd, o_gla, o_lru], axis=-1) @ lp['w_out']
    if ctx is None:
        return out, (k, v, jnp.stack([ssd_f, ssd_b], axis=1), jnp.stack([gla_f, gla_b], axis=1),
                     jnp.stack([lru_f, lru_b], axis=1))
    return out, None


def trunk_layer(x, cond, ctx, lp, alpha):
    mod = (jax.nn.silu(cond) @ lp['mod_w'] + lp['mod_b']).reshape(cond.shape[0], 1, N_MOD, D_MODEL)
    sh1, sc1, g1, sh2, sc2, g2, sh3, sc3, g3 = (mod[:, :, i] for i in range(N_MOD))
    f = swiglu(x * (1 + sc1) + sh1, lp['ffn_w_gate'][0], lp['ffn_w_up'][0], lp['ffn_w_down'][0])
    x = layer_norm(alpha * x + 0.5 * g1 * f, lp['ln_g'][0], lp['ln_b'][0])
    m, new_ctx = mixer(x * (1 + sc2) + sh2, ctx, lp)
    x = layer_norm(alpha * x + g2 * m, lp['ln_g'][1], lp['ln_b'][1])
    f = swiglu(x * (1 + sc3) + sh3, lp['ffn_w_gate'][1], lp['ffn_w_up'][1], lp['ffn_w_down'][1])
    x = layer_norm(alpha * x + 0.5 * g3 * f, lp['ln_g'][2], lp['ln_b'][2])
    return x, new_ctx


def setup_inputs(seed: int = 0) -> dict:
    key = jax.random.key(seed)
    keys = iter(jax.random.split(key, 48))
    f32 = jnp.float32

    def nrm(shape, scale):
        return jax.random.normal(next(keys), shape, f32) * scale

    def unif(shape, lo, hi):
        return jax.random.uniform(next(keys), shape, f32, lo, hi)

    beta = (8.0 * DEPTH) ** -0.25
    dt_init = jnp.exp(unif((DEPTH, 2, SSD_HEADS), math.log(1e-3), math.log(1e-1)))
    a_init = unif((DEPTH, 2, LRU_W), 0.9, 0.999)
    return {
        'x_prompt': nrm((BATCH, SEQ, D_MODEL), 1.0),
        'x_sample': nrm((DEC_BATCH, DEC_SEQ, D_MODEL), 1.0),
        'c': nrm((DEC_BATCH, D_MODEL), 1.0),
        'cache_attn_k': nrm((DEC_BATCH, DEPTH, PAST_LEN, ATT_KV_HEADS, HEAD_DIM), 1.0),
        'cache_attn_v': nrm((DEC_BATCH, DEPTH, PAST_LEN, ATT_KV_HEADS, HEAD_DIM), 1.0),
        'state_ssd': nrm((DEC_BATCH, DEPTH, 2, SSD_HEADS, SSD_HEADDIM, SSD_STATE), 0.5),
        'state_gla': nrm((DEC_BATCH, DEPTH, 2, GLA_HEADS, GLA_DK, GLA_DV), 0.5),
        'state_lru': nrm((DEC_BATCH, DEPTH, 2, LRU_W), 0.5),
        'c_ctx': nrm((D_MODEL,), 1.0),
        'mod_w': nrm((DEPTH, D_MODEL, N_MOD * D_MODEL), D_MODEL ** -0.5),
        'mod_b': nrm((DEPTH, N_MOD * D_MODEL), 0.02),
        'ln_g': 1.0 + nrm((DEPTH, 3, D_MODEL), 0.02),
        'ln_b': nrm((DEPTH, 3, D_MODEL), 0.02),
        'ffn_w_gate': nrm((DEPTH, 2, D_MODEL, D_FF), D_MODEL ** -0.5),
        'ffn_w_up': nrm((DEPTH, 2, D_MODEL, D_FF), D_MODEL ** -0.5),
        'ffn_w_down': nrm((DEPTH, 2, D_FF, D_MODEL), beta * D_FF ** -0.5),
        'w_in': nrm((DEPTH, D_MODEL, IN_W), D_MODEL ** -0.5),
        'w_out': nrm((DEPTH, MIX_W, D_MODEL), beta * MIX_W ** -0.5),
        'q_norm': 1.0 + nrm((DEPTH, HEAD_DIM), 0.02),
        'k_norm': 1.0 + nrm((DEPTH, HEAD_DIM), 0.02),
        'ssd_conv_w': nrm((DEPTH, CONV_W, SSD_CONV_CH), CONV_W ** -0.5),
        'ssd_conv_b': nrm((DEPTH, SSD_CONV_CH), 0.02),
        'ssd_a_log': jnp.log(unif((DEPTH, 2, SSD_HEADS), 1.0, 16.0)),
        'ssd_dt_bias': dt_init + jnp.log(-jnp.expm1(-dt_init)),
        'ssd_d': 1.0 + nrm((DEPTH, 2, SSD_HEADS), 0.02),
        'ssd_norm_w': 1.0 + nrm((DEPTH, GROUP_W), 0.02),
        'gla_gate_w': nrm((DEPTH, 2, GLA_RANK, GLA_K), GLA_RANK ** -0.5),
        'gla_gate_b': nrm((DEPTH, 2, GLA_K), 0.02),
        'gla_norm_w': 1.0 + nrm((DEPTH, GLA_DV), 0.02),
        'lru_conv_w': nrm((DEPTH, CONV_W, LRU_W), CONV_W ** -0.5),
        'lru_conv_b': nrm((DEPTH, LRU_W), 0.02),
        'lru_wa': nrm((DEPTH, 2, LRU_BLOCKS, LRU_BW, LRU_BW), LRU_BW ** -0.5),
        'lru_ba': nrm((DEPTH, 2, LRU_W), 0.02),
        'lru_wx': nrm((DEPTH, 2, LRU_BLOCKS, LRU_BW, LRU_BW), LRU_BW ** -0.5),
        'lru_bx': nrm((DEPTH, 2, LRU_W), 0.02),
        'lru_lam': jnp.log(a_init) - jnp.log1p(-a_init),
    }


def reference(x_prompt, x_sample, c, cache_attn_k, cache_attn_v, state_ssd, state_gla, state_lru, c_ctx,
              mod_w, mod_b, ln_g, ln_b, ffn_w_gate, ffn_w_up, ffn_w_down, w_in, w_out, q_norm, k_norm,
              ssd_conv_w, ssd_conv_b, ssd_a_log, ssd_dt_bias, ssd_d, ssd_norm_w,
              gla_gate_w, gla_gate_b, gla_norm_w,
              lru_conv_w, lru_conv_b, lru_wa, lru_ba, lru_wx, lru_bx, lru_lam):
    alpha = (2.0 * DEPTH) ** 0.25
    layers = [dict(mod_w=mod_w[l], mod_b=mod_b[l], ln_g=ln_g[l], ln_b=ln_b[l],
                   ffn_w_gate=ffn_w_gate[l], ffn_w_up=ffn_w_up[l], ffn_w_down=ffn_w_down[l],
                   w_in=w_in[l], w_out=w_out[l], q_norm=q_norm[l], k_norm=k_norm[l],
                   ssd_conv_w=ssd_conv_w[l], ssd_conv_b=ssd_conv_b[l], ssd_a_log=ssd_a_log[l],
                   ssd_dt_bias=ssd_dt_bias[l], ssd_d=ssd_d[l], ssd_norm_w=ssd_norm_w[l],
                   gla_gate_w=gla_gate_w[l], gla_gate_b=gla_gate_b[l], gla_norm_w=gla_norm_w[l],
                   lru_conv_w=lru_conv_w[l], lru_conv_b=lru_conv_b[l], lru_wa=lru_wa[l], lru_ba=lru_ba[l],
                   lru_wx=lru_wx[l], lru_bx=lru_bx[l], lru_lam=lru_lam[l])
              for l in range(DEPTH)]

    y_prompt = x_prompt
    ctx_states = []
    for l in range(DEPTH):
        y_prompt, st = trunk_layer(y_prompt, c_ctx[None], None, layers[l], alpha)
        ctx_states.append(st)
    new_attn_k = jnp.stack([s[0] for s in ctx_states], axis=1)
    new_attn_v = jnp.stack([s[1] for s in ctx_states], axis=1)
    new_ssd = jnp.stack([s[2] for s in ctx_states], axis=1)
    new_gla = jnp.stack([s[3] for s in ctx_states], axis=1)
    new_lru = jnp.stack([s[4] for s in ctx_states], axis=1)

    y_sample = x_sample
    for l in range(DEPTH):
        cached = (cache_attn_k[:, l], cache_attn_v[:, l], state_ssd[:, l], state_gla[:, l], state_lru[:, l])
        y_sample, _ = trunk_layer(y_sample, c, cached, layers[l], alpha)

    return (y_prompt, y_sample, new_attn_k, new_attn_v, new_ssd, new_gla, new_lru)
```

```python
import numpy as np
from contextlib import ExitStack
import concourse.bass as bass
import concourse.mybir as mybir
from concourse.bass_utils import run_bass_kernel_spmd

F32 = mybir.dt.float32
BF16 = mybir.dt.bfloat16
ALU = mybir.AluOpType
AF = mybir.ActivationFunctionType
AX = mybir.AxisListType

D = 4096
KC = 32
DFF = 11008
FC = 86
NTOK = 2560
TT = 512
NTILE = NTOK // TT
DEPTH = 2
ALPHA = (2.0 * DEPTH) ** 0.25
LN_EPS = 1e-5
RMS_EPS = 1e-6
FF_GROUPS = [(0, 11), (11, 11), (22, 11), (33, 11), (44, 11), (55, 11), (66, 10), (76, 10)]
NCORES = 8

STOP_AFTER = None
MIX_BARRIER = True


class Prog:
    ENGS = ('tensor', 'vector', 'scalar', 'gpsimd', 'sync')
    SEM_MAX = 30000

    def __init__(self, nc, stack):
        self.nc = nc
        self.stack = stack
        self.q = {e: [] for e in self.ENGS}
        self.lastw = {}
        self.readers = {}
        self.cur = {}
        self.waited = {e: {} for e in self.ENGS}
        self.dma_ring = {e: [] for e in ('gpsimd', 'sync')}
        self.dma_idx = {e: 0 for e in ('gpsimd', 'sync')}
        self.nsem = 0
        self.all_tokens = {}
        self.alias = {}

    def _x(self, names):
        out = []
        for n in names:
            a = self.alias.get(n)
            if a is None:
                out.append(n)
            else:
                out.extend(a)
        return out

    def new_sem(self):
        s = self.stack.enter_context(self.nc.semaphore(f"s{self.nsem}"))
        self.nsem += 1
        return s

    def _token(self, eng):
        c = self.cur.get(eng)
        if c is None or c[1] >= self.SEM_MAX:
            c = [self.new_sem(), 0]
            self.cur[eng] = c
        c[1] += 1
        self.all_tokens[id(c[0])] = (c[0], c[1])
        return (c[0], c[1], eng)

    def _deps(self, eng, reads, writes):
        deps = {}

        def add(tok):
            if tok is None:
                return
            sem, val, src = tok
            if eng == 'tensor' and src == 'tensor':
                return
            k = id(sem)
            if k not in deps or deps[k][1] < val:
                deps[k] = (sem, val)
        for r in reads:
            add(self.lastw.get(r))
        for w in writes:
            add(self.lastw.get(w))
            rd = self.readers.get(w)
            if rd:
                for t in rd.values():
                    add(t)
        out = []
        wd = self.waited[eng]
        for k, (sem, val) in deps.items():
            if wd.get(k, 0) >= val:
                continue
            wd[k] = val
            out.append((sem, val))
        return out

    def _commit(self, tok, reads, writes):
        for r in reads:
            rd = self.readers.setdefault(r, {})
            k = id(tok[0])
            if k not in rd or rd[k][1] < tok[1]:
                rd[k] = tok
        for w in writes:
            self.lastw[w] = tok
            self.readers[w] = {}

    def op(self, eng, fn, reads=(), writes=()):
        reads, writes = self._x(reads), self._x(writes)
        waits = self._deps(eng, reads, writes)
        tok = self._token(eng)
        self.q[eng].append((fn, waits, (tok[0], 1)))
        self._commit(tok, reads, writes)
        return tok

    def mm(self, fns, reads=(), writes=()):
        reads, writes = self._x(reads), self._x(writes)
        waits = self._deps('tensor', reads, writes)
        tok = self._token('tensor')
        n = len(fns)
        for i, fn in enumerate(fns):
            self.q['tensor'].append((fn, waits if i == 0 else (), (tok[0], 1) if i == n - 1 else None))
        self._commit(tok, reads, writes)
        return tok

    def dma(self, eng, out, in_, reads=(), writes=()):
        reads, writes = self._x(reads), self._x(writes)
        waits = self._deps(eng, reads, writes)
        ring = self.dma_ring[eng]
        i = self.dma_idx[eng]
        self.dma_idx[eng] += 1
        NS = 12
        if len(ring) < NS:
            ring.append([self.new_sem(), 0])
        slot = ring[i % NS]
        if slot[1] + 16 > self.SEM_MAX:
            k = id(slot[0])
            if self.waited[eng].get(k, 0) < slot[1]:
                waits.append((slot[0], slot[1]))
                self.waited[eng][k] = slot[1]
            slot[0] = self.new_sem()
            slot[1] = 0
        if slot[1] > 0:
            k = id(slot[0])
            if self.waited[eng].get(k, 0) < slot[1]:
                waits.append((slot[0], slot[1]))
                self.waited[eng][k] = slot[1]
        slot[1] += 16
        tok = (slot[0], slot[1], 'dma')
        self.all_tokens[id(slot[0])] = (slot[0], slot[1])
        self.q[eng].append((lambda e, o=out, i_=in_: e.dma_start(out=o, in_=i_), waits, (slot[0], 16)))
        self._commit(tok, reads, writes)
        return tok

    def barrier(self):
        toks = list(self.all_tokens.values())
        for e in self.ENGS:
            waits = []
            wd = self.waited[e]
            for sem, val in toks:
                k = id(sem)
                if wd.get(k, 0) >= val:
                    continue
                wd[k] = val
                waits.append((sem, val))
            if waits:
                self.q[e].append((None, waits, None))

    def emit(self, block):
        for name in self.ENGS:
            items = self.q[name]

            def body(e, items=items):
                for fn, waits, inc in items:
                    for sem, val in waits:
                        e.wait_ge(sem, val)
                    if fn is None:
                        continue
                    ins = fn(e)
                    if inc is not None:
                        ins.then_inc(inc[0], inc[1])
            getattr(block, name)(body)


def tile_w_stationary(w, ncols=128):
    K, N = w.shape
    return np.ascontiguousarray(w.reshape(K // 128, 128, N // ncols, ncols).transpose(2, 1, 0, 3))


def feat_major_vec(v):
    sh = v.shape
    n = sh[-1] // 128
    a = v.reshape(*sh[:-1], n, 128)
    return np.ascontiguousarray(np.moveaxis(a, -1, 0))


class Builder:
    def __init__(self):
        self.nc = bass.Bass("TRN2", target_bir_lowering=False)
        self.stack = ExitStack()
        self.p = Prog(self.nc, self.stack)
        self.n_names = 0

    def dram_in(self, name, shape, dt=F32):
        return self.nc.dram_tensor(name, list(shape), dt, kind="ExternalInput").ap()

    def dram_out(self, name, shape, dt=F32):
        return self.nc.dram_tensor(name, list(shape), dt, kind="ExternalOutput").ap()

    def dram_tmp(self, name, shape, dt=F32):
        return self.nc.dram_tensor(name, list(shape), dt, kind="Internal").ap()

    def sb(self, shape, dt=F32, name=None, stack=None):
        self.n_names += 1
        name = f"{name or 't'}_{self.n_names}"
        return (stack or self.stack).enter_context(self.nc.sbuf_tensor(name, list(shape), dt))

    def psum(self, shape, dt=F32, name=None):
        self.n_names += 1
        name = name or f"ps{self.n_names}"
        return self.stack.enter_context(self.nc.psum_tensor(name, list(shape), dt))


def bc(ap, shape):
    return ap.broadcast_to(list(shape))


class Cfg:
    def __init__(self, ts=2048, fc=FC):
        self.TS = ts
        self.FC = fc
        self.NTOK = 512 + ts
        self.NTILE = self.NTOK // TT
        g, o = [], 0
        while o < fc:
            n = min(11, fc - o)
            g.append((o, n))
            o += n
        self.GROUPS = g
        self.SEQS = [(0, 256, False), (256, 256, False), (512, ts, True)]


TM_COLS = 6912
TM_OFF = dict(aq=0, ak=1024, av=1280, sx=1536, sb=2560, sc=2816, sz=3072, gk=4096, gv=4608, gg=5632, sdt=6656)
FM_ROWS = 3200
FM_OFF = dict(gq=0, gk=512, lx=1024, lg=2048, glr=3072)


def build_program(cfg):
    B = Builder()
    nc, p = B.nc, B.p
    NTOK, NTILE, FCn = cfg.NTOK, cfg.NTILE, cfg.FC
    TS = cfg.TS

    xT = B.dram_in("xT", [D, NTOK])
    condT = B.dram_in("condT", [128, KC, 2])
    modw = [[B.dram_in(f"modw{l}{h}", [144, 128, KC * 128]) for h in range(2)] for l in range(2)]
    modb = B.dram_in("modb", [128, 2, 288])
    lng = B.dram_in("lng", [128, 2, 3, KC])
    lnb = B.dram_in("lnb", [128, 2, 3, KC])
    wg = [[B.dram_in(f"wg{l}{f}", [FCn, 128, KC * 128]) for f in range(2)] for l in range(2)]
    wu = [[B.dram_in(f"wu{l}{f}", [FCn, 128, KC * 128]) for f in range(2)] for l in range(2)]
    wd = [[B.dram_in(f"wd{l}{f}", [16, 128, FCn * 256]) for f in range(2)] for l in range(2)]
    wtm = [B.dram_in(f"wtm{l}", [27, 128, KC * 256]) for l in range(2)]
    wfm = [B.dram_in(f"wfm{l}", [25, 128, KC * 128]) for l in range(2)]
    wout = [B.dram_in(f"wout{l}", [32, 128, KC * 128]) for l in range(2)]
    qkn = B.dram_in("qkn", [128, 2, 1280])
    rope = B.dram_in("rope", [TS, 2, 64])
    ck = B.dram_in("ck", [2, 256, 256])
    cv = B.dram_in("cv", [2, 256, 256])
    sconvw = B.dram_in("sconvw", [128, 2, 4, 1536])
    sconvb = B.dram_in("sconvb", [128, 2, 1536])
    salog = B.dram_in("salog", [128, 2, 32])
    sdtb = B.dram_in("sdtb", [128, 2, 32])
    sdd = B.dram_in("sdd", [128, 2, 2, 16])
    snw = B.dram_in("snw", [128, 2, 1024])
    sst = B.dram_in("sst", [2, 2, 128, 1024])
    ggw = B.dram_in("ggw", [16, 2, 2, 512])
    ggb = B.dram_in("ggb", [128, 2, 2, 512])
    gnw = B.dram_in("gnw", [128, 2, 1024])
    gst = B.dram_in("gst", [2, 2, 128, 1024])
    lcw = B.dram_in("lcw", [128, 2, 8, 4])
    lcb = B.dram_in("lcb", [128, 2, 8])
    lwa = B.dram_in("lwa", [2, 2, 8, 128, 128])
    lwx = B.dram_in("lwx", [2, 2, 8, 128, 128])
    lba = B.dram_in("lba", [128, 2, 2, 8])
    lbx = B.dram_in("lbx", [128, 2, 2, 8])
    llam = B.dram_in("llam", [128, 2, 2, 8])
    lst = B.dram_in("lst", [128, 2, 2, 8])
    consts = B.dram_in("consts", [128, 8, 128])

    yT = B.dram_out("yT", [D, NTOK])
    o_k = B.dram_out("o_k", [2, 2, 256, 256])
    o_v = B.dram_out("o_v", [2, 2, 256, 256])
    o_ssd = B.dram_out("o_ssd", [2, 2, 2, 128, 1024])
    o_gla = B.dram_out("o_gla", [2, 2, 2, 128, 1024])
    o_lru = B.dram_out("o_lru", [2, 128, 2, 2, 8])

    X1 = B.dram_tmp("X1", [D, NTOK])
    PTM = B.dram_tmp("PTM", [NTOK, TM_COLS])
    PFM = B.dram_tmp("PFM", [FM_ROWS, NTOK])
    CTs = B.dram_tmp("CTs", [D, NTOK], BF16)
    YF = B.dram_tmp("YF", [NTOK, 1024])
    OF = B.dram_tmp("OF", [NTOK, 1024])

    ps = [B.psum([128, 512], F32, name=f"psb{i}") for i in range(7)]
    psb = B.psum([128, 1024], BF16, name="psbf")
    PS = lambda i: ("ps", i)

    cst = B.sb([128, 8, 128], F32, "cst")
    cstb = B.sb([128, 8, 128], BF16, "cstb")
    modv = B.sb([128, 2, 288, 2], F32, "modv")
    mod1p = B.sb([128, 2, 288, 2], F32, "mod1p")
    modh = B.sb([128, 2, 288, 2], F32, "modh")
    lngs = B.sb([128, 2, 3, KC], F32, "lngs")
    lnbs = B.sb([128, 2, 3, KC], F32, "lnbs")
    onesD = B.sb([128, 128], F32, "onesD")

    U_, L_, SU_, SL_, I_, ONE_, NEGF_, NEGB_ = range(8)

    t0 = p.dma('sync', cst[:], consts, writes=["cst"])
    p.dma('sync', lngs[:], lng, writes=["lngs"])
    p.dma('sync', lnbs[:], lnb, writes=["lnbs"])
    p.op('vector', lambda e: e.tensor_copy(out=cstb[:], in_=cst[:]), reads=["cst"], writes=["cstb"])
    p.op('vector', lambda e: e.memset(onesD[:], 1.0 / D), writes=["onesD"])

    def mv(tile_, l, m3, kc, ci):
        return tile_[:, l, m3 * 32 + kc, ci:ci + 1]

    with ExitStack() as st:
        cT = B.sb([128, KC, 2], F32, "cT", st)
        scT = B.sb([128, KC, 2], BF16, "scT", st)
        mb = B.sb([128, 2, 288], F32, "mb", st)
        NWS = 4
        wsl = [B.sb([128, KC * 128], BF16, f"mws{i}", st) for i in range(NWS)]
        p.dma('sync', cT[:], condT, writes=["cT"])
        p.dma('sync', mb[:], modb, writes=["mb"])
        p.op('scalar', lambda e: e.activation(out=scT[:], in_=cT[:], func=AF.Silu), reads=["cT"], writes=["scT"])
        it = 0
        for l in range(2):
            for t in range(288):
                s = it % NWS
                it += 1
                src = modw[l][t // 144][t % 144]
                p.dma('gpsimd', wsl[s][:], src, writes=[("mws", s)])
                pbi = (it - 1) % 7
                pb = ps[pbi]
                col = 0
                fns = []
                for kc in range(KC):
                    fns.append(lambda e, s=s, kc=kc, pb=pb, col=col: e.matmul(
                        pb[:, col:col + 2], lhsT=wsl[s][:, kc * 128:(kc + 1) * 128], rhs=scT[:, kc, :],
                        start=(kc == 0), stop=(kc == KC - 1)))
                p.mm(fns, reads=[("mws", s), "scT"], writes=[PS((it - 1) % 7)])
                p.op('vector', lambda e, l=l, t=t, pb=pb, col=col: e.tensor_tensor(
                    out=modv[:, l, t, :], in0=pb[:, col:col + 2], in1=bc(mb[:, l, t:t + 1], [128, 2]), op=ALU.add),
                    reads=[PS((it - 1) % 7), "mb"], writes=["modv"])
        p.op('vector', lambda e: e.tensor_scalar_add(out=mod1p[:], in0=modv[:], scalar1=1.0), reads=["modv"], writes=["mod1p"])
        p.op('vector', lambda e: e.tensor_scalar_mul(out=modh[:], in0=modv[:], scalar1=0.5), reads=["modv"], writes=["modh"])
        p.barrier()
    X1v = X1.rearrange("(k p) t -> p k t", p=128)
    CTv = CTs.rearrange("(k p) t -> p k t", p=128)
    xTv = xT.rearrange("(k p) t -> p k t", p=128)
    yTv = yT.rearrange("(k p) t -> p k t", p=128)
    ctr = dict(gu=0, pg=0, sg=0, dn=0, pd=0, sq=0, pp=0, stg=0)
    NG = len(cfg.GROUPS)

    def tile_stage(stage):
        with ExitStack() as st:
            xres = B.sb([128, KC, TT], F32, "xres", st)
            xmod = B.sb([128, KC, TT], BF16, "xmod", st)
            hb = [B.sb([128, 11, TT], BF16, f"hb{i}", st) for i in range(2)]
            NGU, NDN = 2, 3
            wgu = [B.sb([128, 2, KC * 128], BF16, f"wgu{i}", st) for i in range(NGU)]
            wdn = [B.sb([128, 11 * 256], BF16, f"wdn{i}", st) for i in range(NDN)]
            sg = [B.sb([128, TT], F32, f"sg{i}", st) for i in range(2)]
            sq = [B.sb([128, TT], F32, f"sq{i}", st) for i in range(2)]
            stg = [B.sb([128, TT], F32, f"stg{i}", st) for i in range(2)]
            mean = B.sb([128, TT], F32, "mean", st)
            rstd = B.sb([128, TT], F32, "rstd", st)
            var = B.sb([128, TT], F32, "var", st)
            XR = [("xres", kc) for kc in range(KC)]
            XM = [("xmod", kc) for kc in range(KC)]

            def modulate(l, m, ci):
                for kc in range(KC):
                    if kc % 2 == 0:
                        p.op('vector', lambda e, kc=kc: e.tensor_scalar(
                            out=xmod[:, kc, :], in0=xres[:, kc, :], scalar1=mv(mod1p, l, 3 * m + 1, kc, ci),
                            scalar2=mv(modv, l, 3 * m, kc, ci), op0=ALU.mult, op1=ALU.add),
                            reads=[("xres", kc)], writes=[("xmod", kc)])
                    else:
                        p.op('scalar', lambda e, kc=kc: e.activation(
                            out=xmod[:, kc, :], in_=xres[:, kc, :], func=AF.Identity,
                            bias=mv(modv, l, 3 * m, kc, ci), scale=mv(mod1p, l, 3 * m + 1, kc, ci)),
                            reads=[("xres", kc)], writes=[("xmod", kc)])

            def scale_alpha():
                for kc in range(KC):
                    if kc % 2 == 1:
                        p.op('vector', lambda e, kc=kc: e.tensor_scalar_mul(out=xres[:, kc, :], in0=xres[:, kc, :], scalar1=ALPHA),
                             reads=[("xres", kc)], writes=[("xres", kc)])
                    else:
                        p.op('scalar', lambda e, kc=kc: e.mul(out=xres[:, kc, :], in_=xres[:, kc, :], mul=ALPHA),
                             reads=[("xres", kc)], writes=[("xres", kc)])

            def ffn(l, f, ci):
                m = 0 if f == 0 else 2
                scale_alpha()

                def gateup(q):
                    c0, n = cfg.GROUPS[q]
                    for kk in range(n):
                        c = c0 + kk
                        s = ctr['gu'] % NGU
                        ctr['gu'] += 1
                        p.dma('gpsimd', wgu[s][:, 0, :], wg[l][f][c], writes=[("wgu", s, 0)])
                        p.dma('gpsimd', wgu[s][:, 1, :], wu[l][f][c], writes=[("wgu", s, 1)])
                        pg = ctr['pg'] % 2
                        ctr['pg'] += 1
                        bg, bu = ps[2 * pg], ps[2 * pg + 1]
                        p.mm([lambda e, kc=kc, s=s, bg=bg: e.matmul(bg[:], lhsT=wgu[s][:, 0, kc * 128:(kc + 1) * 128], rhs=xmod[:, kc, :],
                                                                 start=(kc == 0), stop=(kc == KC - 1)) for kc in range(KC)],
                             reads=[("wgu", s, 0)] + XM, writes=[PS(2 * pg)])
                        p.mm([lambda e, kc=kc, s=s, bu=bu: e.matmul(bu[:], lhsT=wgu[s][:, 1, kc * 128:(kc + 1) * 128], rhs=xmod[:, kc, :],
                                                                 start=(kc == 0), stop=(kc == KC - 1)) for kc in range(KC)],
                             reads=[("wgu", s, 1)] + XM, writes=[PS(2 * pg + 1)])
                        si = ctr['sg'] % 2
                        ctr['sg'] += 1
                        p.op('scalar', lambda e, si=si, bg=bg: e.activation(out=sg[si][:], in_=bg[:], func=AF.Silu),
                             reads=[PS(2 * pg)], writes=[("sg", si)])
                        p.op('vector', lambda e, si=si, bu=bu, q=q, kk=kk: e.tensor_tensor(out=hb[q % 2][:, kk, :], in0=sg[si][:], in1=bu[:], op=ALU.mult),
                             reads=[("sg", si), PS(2 * pg + 1)], writes=[("hb", q % 2, kk)])

                def down(q):
                    c0, n = cfg.GROUPS[q]
                    for dp in range(16):
                        s = ctr['dn'] % NDN
                        ctr['dn'] += 1
                        p.dma('gpsimd', wdn[s][:, 0:n * 256], wd[l][f][dp][:, c0 * 256:(c0 + n) * 256], writes=[("wdn", s)])
                        for half in range(2):
                            dc = dp * 2 + half
                            pb = 4 + ctr['pd'] % 2
                            ctr['pd'] += 1
                            p.mm([lambda e, kk=kk, s=s, pb=pb, half=half, q=q: e.matmul(
                                ps[pb][:], lhsT=wdn[s][:, kk * 256 + half * 128: kk * 256 + half * 128 + 128], rhs=hb[q % 2][:, kk, :],
                                start=(kk == 0), stop=(kk == n - 1)) for kk in range(n)],
                                reads=[("wdn", s)] + [("hb", q % 2, kk) for kk in range(n)], writes=[PS(pb)])
                            p.op('vector', lambda e, pb=pb, dc=dc: e.scalar_tensor_tensor(
                                out=xres[:, dc, :], in0=ps[pb][:], scalar=mv(modh, l, 3 * m + 2, dc, ci), in1=xres[:, dc, :],
                                op0=ALU.mult, op1=ALU.add), reads=[PS(pb), ("xres", dc)], writes=[("xres", dc)])
                for q in range(NG):
                    gateup(q)
                    if q > 0:
                        down(q - 1)
                down(NG - 1)

            def ln(l, j, ci, nxt):
                for kc in range(KC):
                    si = ctr['sq'] % 2
                    ctr['sq'] += 1
                    p.op('scalar', lambda e, kc=kc, si=si: e.activation(out=sq[si][:], in_=xres[:, kc, :], func=AF.Square),
                         reads=[("xres", kc)], writes=[("sq", si)])
                    p.mm([lambda e, kc=kc: e.matmul(ps[4][:], lhsT=onesD[:], rhs=xres[:, kc, :], start=(kc == 0), stop=(kc == KC - 1))],
                         reads=[("xres", kc), "onesD"], writes=[PS(4)])
                    p.mm([lambda e, kc=kc, si=si: e.matmul(ps[5][:], lhsT=onesD[:], rhs=sq[si][:], start=(kc == 0), stop=(kc == KC - 1))],
                         reads=[("sq", si), "onesD"], writes=[PS(5)])
                p.op('vector', lambda e: e.tensor_copy(out=mean[:], in_=ps[4][:]), reads=[PS(4)], writes=["mean"])
                p.op('vector', lambda e: e.tensor_tensor(out=var[:], in0=mean[:], in1=mean[:], op=ALU.mult), reads=["mean"], writes=["var"])
                p.op('vector', lambda e: e.tensor_tensor(out=var[:], in0=ps[5][:], in1=var[:], op=ALU.subtract), reads=[PS(5), "var"], writes=["var"])
                p.op('vector', lambda e: e.tensor_scalar(out=var[:], in0=var[:], scalar1=0.0, scalar2=LN_EPS, op0=ALU.max, op1=ALU.add), reads=["var"], writes=["var"])
                p.op('scalar', lambda e: e.activation(out=rstd[:], in_=var[:], func=AF.Sqrt), reads=["var"], writes=["rstd"])
                p.op('vector', lambda e: e.reciprocal(out=rstd[:], in_=rstd[:]), reads=["rstd"], writes=["rstd"])
                for kc in range(KC):
                    p.op('vector', lambda e, kc=kc: e.tensor_tensor(out=xres[:, kc, :], in0=xres[:, kc, :], in1=mean[:], op=ALU.subtract),
                         reads=[("xres", kc), "mean"], writes=[("xres", kc)])
                    p.op('vector', lambda e, kc=kc: e.tensor_tensor(out=xres[:, kc, :], in0=xres[:, kc, :], in1=rstd[:], op=ALU.mult),
                         reads=[("xres", kc), "rstd"], writes=[("xres", kc)])
                    p.op('scalar', lambda e, kc=kc: e.activation(out=xres[:, kc, :], in_=xres[:, kc, :], func=AF.Identity,
                                                                 bias=lnbs[:, l, j, kc:kc + 1], scale=lngs[:, l, j, kc:kc + 1]),
                         reads=[("xres", kc)], writes=[("xres", kc)])
                if nxt is not None:
                    modulate(nxt[0], nxt[1], ci)

            def evac(pb, width=TT):
                g = ctr["stg"] % 2
                ctr['stg'] += 1
                if g % 2 == 0:
                    p.op('vector', lambda e: e.tensor_copy(out=stg[g][:, 0:width], in_=ps[pb][:, 0:width]), reads=[PS(pb)], writes=[("stg", g)])
                else:
                    p.op('scalar', lambda e: e.copy(out=stg[g][:, 0:width], in_=ps[pb][:, 0:width]), reads=[PS(pb)], writes=[("stg", g)])
                return g

            def proj(l, tile):
                c0 = tile * TT
                for t in range(25):
                    s = ctr['gu'] % NGU
                    ctr['gu'] += 1
                    p.dma('gpsimd', wgu[s][:, 0, :], wfm[l][t], writes=[("wgu", s, 0)])
                    pb = ctr['pp'] % 4
                    ctr['pp'] += 1
                    p.mm([lambda e, kc=kc, s=s, pb=pb: e.matmul(ps[pb][:], lhsT=wgu[s][:, 0, kc * 128:(kc + 1) * 128], rhs=xmod[:, kc, :],
                                                             start=(kc == 0), stop=(kc == KC - 1)) for kc in range(KC)],
                         reads=[("wgu", s, 0)] + XM, writes=[PS(pb)])
                    g = evac(pb)
                    p.dma('sync', PFM[t * 128:(t + 1) * 128, c0:c0 + TT], stg[g][:], reads=[("stg", g)], writes=[("PFM", tile, t)])
                for gi in range(27):
                    s = ctr['gu'] % NGU
                    ctr['gu'] += 1
                    wflat = wgu[s][:].rearrange("p a b -> p (a b)")
                    p.dma('gpsimd', wflat, wtm[l][gi], writes=[("wgu", s, 0), ("wgu", s, 1)])
                    for tb in range(4):
                        pb = ctr['pp'] % 4
                        ctr['pp'] += 1
                        p.mm([lambda e, kc=kc, wflat=wflat, pb=pb, tb=tb: e.matmul(
                            ps[pb][:, 0:256], lhsT=xmod[:, kc, tb * 128:(tb + 1) * 128], rhs=wflat[:, kc * 256:(kc + 1) * 256],
                            start=(kc == 0), stop=(kc == KC - 1)) for kc in range(KC)],
                            reads=[("wgu", s, 0), ("wgu", s, 1)] + XM, writes=[PS(pb)])
                        g = evac(pb, 256)
                        p.dma('sync', PTM[c0 + tb * 128:c0 + (tb + 1) * 128, gi * 256:(gi + 1) * 256], stg[g][:, 0:256],
                              reads=[("stg", g)], writes=[("PTM", tile, gi, tb)])

            def outproj(l, tile, ci):
                c0 = tile * TT
                p.dma('sync', xres[:], X1v[:, :, c0:c0 + TT], writes=XR)
                p.dma('sync', xmod[:], CTv[:, :, c0:c0 + TT], writes=XM)
                scale_alpha()
                for t in range(32):
                    s = ctr['gu'] % NGU
                    ctr['gu'] += 1
                    p.dma('gpsimd', wgu[s][:, 0, :], wout[l][t], writes=[("wgu", s, 0)])
                    pb = ctr['pp'] % 4
                    ctr['pp'] += 1
                    p.mm([lambda e, kc=kc, s=s, pb=pb: e.matmul(ps[pb][:], lhsT=wgu[s][:, 0, kc * 128:(kc + 1) * 128], rhs=xmod[:, kc, :],
                                                             start=(kc == 0), stop=(kc == KC - 1)) for kc in range(KC)],
                         reads=[("wgu", s, 0)] + XM, writes=[PS(pb)])
                    p.op('vector', lambda e, pb=pb, t=t: e.scalar_tensor_tensor(
                        out=xres[:, t, :], in0=ps[pb][:], scalar=mv(modv, l, 5, t, ci), in1=xres[:, t, :],
                        op0=ALU.mult, op1=ALU.add), reads=[PS(pb), ("xres", t)], writes=[("xres", t)])

            for tile in range(NTILE):
                ci = 0 if tile == 0 else 1
                c0 = tile * TT
                if stage == 0:
                    p.dma('sync', xres[:], xTv[:, :, c0:c0 + TT], writes=XR)
                    modulate(0, 0, ci)
                    ffn(0, 0, ci)
                    ln(0, 0, ci, (0, 1))
                    p.dma('sync', X1v[:, :, c0:c0 + TT], xres[:], reads=XR, writes=[("X1", tile)])
                    proj(0, tile)
                elif stage == 1:
                    outproj(0, tile, ci)
                    ln(0, 1, ci, (0, 2))
                    ffn(0, 1, ci)
                    ln(0, 2, ci, (1, 0))
                    ffn(1, 0, ci)
                    ln(1, 0, ci, (1, 1))
                    p.dma('sync', X1v[:, :, c0:c0 + TT], xres[:], reads=XR, writes=[("X1", tile)])
                    proj(1, tile)
                else:
                    outproj(1, tile, ci)
                    ln(1, 1, ci, (1, 2))
                    ffn(1, 1, ci)
                    ln(1, 2, ci, None)
                    p.dma('sync', yTv[:, :, c0:c0 + TT], xres[:], reads=XR, writes=[("yT", tile)])
            p.barrier()
    lruo = B.sb([128, 2, 2, 2, 8], F32, "lruo")
    mc = dict(qkv=0, as_=0, pt=0, ao=0, ot=0)
    SCALE = 128.0 ** -0.5

    def mixer_stage(l):
        with ExitStack() as st:
            TMAX = TS
            NBmax = TMAX // 128
            qknw = B.sb([128, 1280], F32, "qknw", st)
            cw = B.sb([128, 4, 1536], F32, "cw", st)
            cb = B.sb([128, 1536], F32, "cb", st)
            dtb = B.sb([128, 32], F32, "dtb", st)
            aneg = B.sb([128, 32], F32, "aneg", st)
            dsum = B.sb([128, 16], F32, "dsum", st)
            sdt_ = B.sb([128, 2, 16], F32, "sdt_", st)
            snwt = B.sb([128, 1024], F32, "snwt", st)
            ggwt = B.sb([16, 2, 512], F32, "ggwt", st)
            ggbt = B.sb([128, 2, 512], F32, "ggbt", st)
            gnwt = B.sb([128, 1024], F32, "gnwt", st)
            lcwt = B.sb([128, 8, 4], F32, "lcwt", st)
            lcbt = B.sb([128, 8], F32, "lcbt", st)
            lbat = B.sb([128, 2, 8], F32, "lbat", st)
            lbxt = B.sb([128, 2, 8], F32, "lbxt", st)
            c8 = B.sb([128, 2, 8], F32, "c8", st)
            c16 = B.sb([128, 2, 8], F32, "c16", st)
            lstt = B.sb([128, 2, 8], F32, "lstt", st)
            PR = "mparams"
            for dst, src in ((qknw[:], qkn[:, l, :]), (cw[:], sconvw[:, l]), (cb[:], sconvb[:, l]), (dtb[:], sdtb[:, l]),
                             (aneg[:], salog[:, l]), (sdt_[:], sdd[:, l]), (snwt[:], snw[:, l]), (ggwt[:], ggw[:, l]),
                             (ggbt[:], ggb[:, l]), (gnwt[:], gnw[:, l]), (lcwt[:], lcw[:, l]), (lcbt[:], lcb[:, l]),
                             (lbat[:], lba[:, l]), (lbxt[:], lbx[:, l]), (c8[:], llam[:, l]), (lstt[:], lst[:, l])):
                p.dma('sync', dst, src, writes=[PR])
            p.op('scalar', lambda e: e.activation(out=aneg[:], in_=aneg[:], func=AF.Exp), reads=[PR], writes=[PR])
            p.op('vector', lambda e: e.tensor_scalar_mul(out=aneg[:], in0=aneg[:], scalar1=-1.0), reads=[PR], writes=[PR])
            p.op('vector', lambda e: e.tensor_tensor(out=dsum[:], in0=sdt_[:, 0, :], in1=sdt_[:, 1, :], op=ALU.add), reads=[PR], writes=[PR])
            p.op('scalar', lambda e: e.activation(out=c8[:], in_=c8[:], func=AF.Exp, scale=-1.0), reads=[PR], writes=[PR])
            p.op('scalar', lambda e: e.activation(out=c8[:], in_=c8[:], func=AF.Ln, bias=1.0), reads=[PR], writes=[PR])
            p.op('vector', lambda e: e.tensor_scalar_mul(out=c16[:], in0=c8[:], scalar1=-16.0), reads=[PR], writes=[PR])
            p.op('vector', lambda e: e.tensor_scalar_mul(out=c8[:], in0=c8[:], scalar1=-8.0), reads=[PR], writes=[PR])
            p.barrier()

            big = [B.sb([128, 2048], F32, f"big{i}", st) for i in range(9)]
            xpt = B.sb([128, 2056], F32, "xpt", st)
            big.append(xpt)
            b16 = [B.sb([128, 2048], BF16, f"b16_{i}", st) for i in range(6)]
            sm = B.sb([128, 256], F32, "sm", st)
            KT = B.sb([128, 2, TMAX + 256], BF16, "KT", st)
            Vtm = B.sb([128, NBmax + 2, 256], BF16, "Vtm", st)
            QTt = [big[5 + i][:, 0:TMAX].bitcast(BF16) for i in range(4)]

            def QTh(h, c0, c1):
                return QTt[h // 2][:, (h % 2) * TMAX + c0:(h % 2) * TMAX + c1]
            B16H = lambda i: [("b16", i, 0), ("b16", i, 1)]
            ALIAS_ATT = {"qkv": [("big", 0)], "sqt": [("big", 1)], "ropetmp": [("big", 2), ("big", 3)], "rz": [("big", 4)],
                         "qkb": [("b16", 0)], ("pt", 0): [("b16", 1, 0)], ("pt", 1): [("b16", 1, 1)],
                         ("ot", 0): [("b16", 2, 0)], ("ot", 1): [("b16", 2, 1)], "ss": ["sm"],
                         "QT": [("big", 5), ("big", 6), ("big", 7), ("big", 8)]}
            ALIAS_SSD = {("xin", 0): [("big", 0)], ("xin", 1): [("big", 1)], ("xin", 2): [("big", 2)], ("xin", 3): [("big", 3)],
                         "acc": [("big", 4)], "seg": [("big", 5)], "rhs2": [("big", 6)], "ydir": [("big", 7)], "yf": [("big", 8)],
                         "zt": [("big", 9)], "bcb": [("b16", 0)], "xdt": B16H(1), "xde": B16H(2), "sc": [("b16", 3)],
                         "BCT": [("b16", 4)], "ob16": [("b16", 5)], "dts": ["sm"], "la": ["sm"], "nla": ["sm"], "ex": ["sm"], "ss1": ["sm"]}
            ALIAS_GLA = {"gqT": [("big", 0)], "gkT": [("big", 1)], "gktm": [("big", 1)], "gt": [("big", 2)], "eb": [("big", 3)],
                         "enb": [("big", 4)], "et": [("big", 4)], "od": [("big", 5)], "of_": [("big", 6)], "ggt": [("big", 7)],
                         "gvf": [("big", 8)], "S": [("big", 9)], "qtb": [("b16", 0)], "ktb": B16H(1), "khat": B16H(2),
                         "gvb": [("b16", 3)], "attb": [("b16", 4)], "Sb": [("b16", 5)], "glrT": ["rp"], "ss4": ["sm"]}
            ALIAS_LRU = {"xl": [("big", 0)], "rt": [("big", 1)], "it": [("big", 2)], "at": [("big", 3)], "ut": [("big", 4)],
                         "hf": [("big", 5)], "hb": [("big", 6)], "lgT": [("big", 7)], "tmp": [("big", 8)], "xp": [("big", 9)],
                         "xlb": [("b16", 0)], "ob": B16H(1), "wab": B16H(2), "wxb": B16H(2)}
            hT = B.sb([128, 1024], F32, "hT", st)
            hTb = B.sb([128, 1024], BF16, "hTb", st)
            rp = B.sb([128, 2, 64], F32, "rp", st)
            oT = B.sb([128, 8, 128], BF16, "oT", st)

            def emit_T(src_bf, rows0, c0, tag):
                p.mm([lambda e, i=i: e.transpose(psb[:, i * 128:(i + 1) * 128], src_bf[:, i * 128:(i + 1) * 128], cstb[:, I_, :]) for i in range(8)],
                     reads=[tag], writes=["psb"])
                p.op('vector', lambda e: e.tensor_copy(out=oT[:], in_=psb[:].rearrange("p (i t) -> p i t", i=8)), reads=["psb"], writes=["oT"])
                p.dma('sync', CTs[rows0:rows0 + 1024, c0:c0 + 128].rearrange("(i p) t -> p i t", p=128), oT[:], reads=["oT"], writes=[("CT", rows0, c0)])

            def attention(si, s0, T, is_s):
                p.alias = ALIAS_ATT
                NB = T // 128
                NKB = NB + (2 if is_s else 0)
                qkv, sqt, qkb = big[0], big[1], b16[0]
                ss = sm
                for b in range(NB):
                    r0 = s0 + b * 128
                    p.dma('sync', qkv[:, 0:1536], PTM[r0:r0 + 128, 0:1536], writes=["qkv"])
                    p.op('scalar', lambda e: e.activation(out=sqt[:, 0:1280], in_=qkv[:, 0:1280], func=AF.Square), reads=["qkv"], writes=["sqt"])
                    p.op('vector', lambda e: e.tensor_reduce(out=ss[:, 0:10], in_=sqt[:, 0:1280].rearrange("p (h d) -> p h d", d=128), axis=AX.X, op=ALU.add),
                         reads=["sqt"], writes=["ss"])
                    p.op('vector', lambda e: e.tensor_scalar(out=ss[:, 0:10], in0=ss[:, 0:10], scalar1=1.0 / 128, scalar2=RMS_EPS, op0=ALU.mult, op1=ALU.add),
                         reads=["ss"], writes=["ss"])
                    p.op('scalar', lambda e: e.activation(out=ss[:, 0:10], in_=ss[:, 0:10], func=AF.Sqrt), reads=["ss"], writes=["ss"])
                    p.op('vector', lambda e: e.reciprocal(out=ss[:, 0:10], in_=ss[:, 0:10]), reads=["ss"], writes=["ss"])
                    qk3 = qkv[:, 0:1280].rearrange("p (h d) -> p h d", d=128)
                    p.op('vector', lambda e: e.tensor_tensor(out=qk3, in0=qk3, in1=bc(ss[:, 0:10].unsqueeze(2), [128, 10, 128]), op=ALU.mult),
                         reads=["qkv", "ss"], writes=["qkv"])
                    p.op('vector', lambda e: e.tensor_tensor(out=qkv[:, 0:1280], in0=qkv[:, 0:1280], in1=qknw[:], op=ALU.mult), reads=["qkv"], writes=["qkv"])
                    if is_s:
                        p.dma('sync', rp[:], rope[b * 128:(b + 1) * 128], writes=["rp"])
                        x5 = qkv[:, 0:1280].rearrange("p (h a t f) -> p h a t f", h=10, a=2, t=2, f=32)
                        x1, x2 = x5[:, :, :, 0, :], x5[:, :, :, 1, :]
                        cosb = bc(rp[:, 0, :].rearrange("p (a f) -> p a f", a=2).unsqueeze(1), [128, 10, 2, 32])
                        sinb = bc(rp[:, 1, :].rearrange("p (a f) -> p a f", a=2).unsqueeze(1), [128, 10, 2, 32])
                        ta = big[2][:, 0:640].rearrange("p (h a f) -> p h a f", h=10, a=2)
                        tb_ = big[2][:, 640:1280].rearrange("p (h a f) -> p h a f", h=10, a=2)
                        tc_ = big[3][:, 0:640].rearrange("p (h a f) -> p h a f", h=10, a=2)
                        td = big[3][:, 640:1280].rearrange("p (h a f) -> p h a f", h=10, a=2)
                        for o_, i0, i1 in ((ta, x1, cosb), (tb_, x2, sinb), (tc_, x2, cosb), (td, x1, sinb)):
                            p.op('vector', lambda e, o_=o_, i0=i0, i1=i1: e.tensor_tensor(out=o_, in0=i0, in1=i1, op=ALU.mult),
                                 reads=["qkv", "rp"], writes=["ropetmp"])
                        p.op('vector', lambda e: e.tensor_tensor(out=x1, in0=ta, in1=tb_, op=ALU.subtract), reads=["ropetmp"], writes=["qkv"])
                        p.op('vector', lambda e: e.tensor_tensor(out=x2, in0=tc_, in1=td, op=ALU.add), reads=["ropetmp"], writes=["qkv"])
                    else:
                        p.dma('sync', o_k[si, l, b * 128:(b + 1) * 128, :], qkv[:, 1024:1280], reads=["qkv"], writes=[("ok", si, b)])
                        p.dma('sync', o_v[si, l, b * 128:(b + 1) * 128, :], qkv[:, 1280:1536], reads=["qkv"], writes=[("ov", si, b)])
                    p.op('scalar', lambda e: e.copy(out=qkb[:, 0:1536], in_=qkv[:, 0:1536]), reads=["qkv"], writes=["qkb"])
                    p.mm([lambda e, h=h: e.transpose(psb[:, h * 128:(h + 1) * 128], qkb[:, h * 128:(h + 1) * 128], cstb[:, I_, :]) for h in range(8)],
                         reads=["qkb"], writes=["psb"])
                    for i4 in range(4):
                        p.op('vector', lambda e, b=b, i4=i4: e.tensor_copy(
                            out=QTt[i4].rearrange("p (h t) -> p h t", h=2)[:, :, b * 128:(b + 1) * 128],
                            in_=psb[:, i4 * 256:(i4 + 1) * 256].rearrange("p (h t) -> p h t", h=2)), reads=["psb"], writes=["QT"])
                    p.mm([lambda e, g=g: e.transpose(psb[:, g * 128:(g + 1) * 128], qkb[:, 1024 + g * 128:1024 + (g + 1) * 128], cstb[:, I_, :]) for g in range(2)],
                         reads=["qkb"], writes=["psb"])
                    p.op('vector', lambda e, b=b: e.tensor_copy(out=KT[:, :, b * 128:(b + 1) * 128], in_=psb[:, 0:256].rearrange("p (g t) -> p g t", g=2)),
                         reads=["psb"], writes=["KT"])
                    p.op('scalar', lambda e, b=b: e.copy(out=Vtm[:, b, :], in_=qkb[:, 1280:1536]), reads=["qkb"], writes=["Vtm"])
                if is_s:
                    for cbk in range(2):
                        p.dma('sync', qkv[:, 0:256], ck[l, cbk * 128:(cbk + 1) * 128, :], writes=["qkv"])
                        p.dma('sync', qkv[:, 256:512], cv[l, cbk * 128:(cbk + 1) * 128, :], writes=["qkv"])
                        p.op('scalar', lambda e: e.copy(out=qkb[:, 0:512], in_=qkv[:, 0:512]), reads=["qkv"], writes=["qkb"])
                        p.mm([lambda e, g=g: e.transpose(psb[:, g * 128:(g + 1) * 128], qkb[:, g * 128:(g + 1) * 128], cstb[:, I_, :]) for g in range(2)],
                             reads=["qkb"], writes=["psb"])
                        p.op('vector', lambda e, cbk=cbk: e.tensor_copy(out=KT[:, :, T + cbk * 128:T + (cbk + 1) * 128],
                                                                       in_=psb[:, 0:256].rearrange("p (g t) -> p g t", g=2)), reads=["psb"], writes=["KT"])
                        p.op('scalar', lambda e, cbk=cbk: e.copy(out=Vtm[:, NB + cbk, :], in_=qkb[:, 256:512]), reads=["qkb"], writes=["Vtm"])
                pt = [b16[1][:, 0:512], b16[1][:, 512:1024]]
                ot = [b16[2][:, 0:512], b16[2][:, 512:1024]]
                rz = big[4][:, 0:512]
                def att_group(g):
                    for grp in range(4 * T // 512):
                        ob = 2 + mc['ao'] % 2
                        mc['ao'] += 1
                        col0 = g * 4 * TMAX + 0
                        hq = (grp * 512) // T
                        t0 = (grp * 512) % T
                        for kc in range(NKB):
                            sb_ = mc['as_'] % 2
                            mc['as_'] += 1
                            if T >= 512:
                                rhs = QTh(4 * g + hq, t0, t0 + 512)
                                oap = ps[sb_][:]
                            else:
                                h0 = 4 * g + 2 * grp
                                rhs = QTt[h0 // 2].rearrange("p (h t) -> p h t", h=2)[:, :, 0:T]
                                oap = ps[sb_][:].rearrange("p (h t) -> p h t", h=2)
                            p.mm([lambda e, sb_=sb_, kc=kc, rhs=rhs, oap=oap: e.matmul(oap, lhsT=KT[:, g, kc * 128:(kc + 1) * 128], rhs=rhs, start=True, stop=True)],
                                 reads=["KT", "QT"], writes=[PS(sb_)])
                            pi = mc['pt'] % 2
                            mc['pt'] += 1
                            p.op('scalar', lambda e, pi=pi, sb_=sb_: e.activation(out=pt[pi], in_=ps[sb_][:], func=AF.Exp, scale=SCALE),
                                 reads=[PS(sb_)], writes=[("pt", pi)])
                            p.mm([lambda e, pi=pi, kc=kc, ob=ob: e.matmul(ps[ob][:], lhsT=Vtm[:, kc, g * 128:(g + 1) * 128], rhs=pt[pi], start=(kc == 0), stop=(kc == NKB - 1)),
                                  lambda e, pi=pi, kc=kc, ob=ob: e.matmul(ps[ob + 2][:], lhsT=cstb[:, ONE_, :], rhs=pt[pi], start=(kc == 0), stop=(kc == NKB - 1))],
                                 reads=[("pt", pi), "Vtm"], writes=[PS(ob), PS(ob + 2)])
                        p.op('vector', lambda e, ob=ob: e.reciprocal(out=rz, in_=ps[ob + 2][:]), reads=[PS(ob + 2)], writes=["rz"])
                        oi = mc['ot'] % 2
                        mc['ot'] += 1
                        p.op('vector', lambda e, ob=ob, oi=oi: e.tensor_tensor(out=ot[oi], in0=ps[ob][:], in1=rz, op=ALU.mult),
                             reads=[PS(ob), "rz"], writes=[("ot", oi)])
                        if T >= 512:
                            h = 4 * g + hq
                            p.dma('sync', CTs[h * 128:(h + 1) * 128, s0 + t0:s0 + t0 + 512], ot[oi], reads=[("ot", oi)], writes=[("CTa", h, t0)])
                        else:
                            for hh in range(2):
                                h = 4 * g + 2 * grp + hh
                                p.dma('sync', CTs[h * 128:(h + 1) * 128, s0:s0 + T], ot[oi][:, hh * T:(hh + 1) * T], reads=[("ot", oi)], writes=[("CTa", h, hh)])
                for g_ in range(2):
                    att_group(g_)

            def ssd(si, s0, T, is_s):
                p.alias = ALIAS_SSD
                NB = T // 128
                xin = [big[0], big[1], big[2], big[3]]
                acc, seg, rhs2, ydir, yf, zt = big[4], big[5], big[6], big[7], big[8], big[9]
                bcb, xdt, xde, sc_, BCT, ob16 = b16[0], b16[1], b16[2], b16[3], b16[4], b16[5]
                dts, la, nla, ex, ss1 = sm[:, 0:32], sm[:, 32:64], sm[:, 64:96], sm[:, 96:144], sm[:, 144:146]
                def ssd_dir(d):
                    Uinc = cst[:, U_ if d == 0 else L_, :]
                    Sexc = cst[:, SL_ if d == 0 else SU_, :]
                    negm = cst[:, NEGF_ if d == 0 else NEGB_, :]
                    if is_s:
                        p.dma('sync', hT[:], sst[l, d], writes=["hT"])
                    else:
                        p.op('vector', lambda e: e.memset(hT[:], 0.0), writes=["hT"])
                    p.op('scalar', lambda e: e.copy(out=hTb[:], in_=hT[:]), reads=["hT"], writes=["hTb"])
                    for b in (range(NB) if d == 0 else range(NB - 1, -1, -1)):
                        r0 = s0 + b * 128
                        for j in range(4):
                            rs = r0 + j - 2
                            lo, hi = max(s0, rs), min(s0 + T, rs + 128)
                            if hi - lo < 128:
                                p.op('vector', lambda e, j=j: e.memset(xin[j][:, 0:1536], 0.0), writes=[("xin", j)])
                            p.dma('sync', xin[j][lo - rs:hi - rs, 0:1536], PTM[lo:hi, 1536:3072], writes=[("xin", j)])
                        p.op('vector', lambda e: e.tensor_tensor(out=acc[:, 0:1536], in0=xin[0][:, 0:1536], in1=cw[:, 0, :], op=ALU.mult), reads=[("xin", 0)], writes=["acc"])
                        for j in range(1, 4):
                            p.op('vector', lambda e, j=j: e.tensor_tensor(out=xin[j][:, 0:1536], in0=xin[j][:, 0:1536], in1=cw[:, j, :], op=ALU.mult),
                                 reads=[("xin", j)], writes=[("xin", j)])
                            p.op('vector', lambda e, j=j: e.tensor_tensor(out=acc[:, 0:1536], in0=acc[:, 0:1536], in1=xin[j][:, 0:1536], op=ALU.add),
                                 reads=[("xin", j), "acc"], writes=["acc"])
                        p.op('vector', lambda e: e.tensor_tensor(out=acc[:, 0:1536], in0=acc[:, 0:1536], in1=cb[:], op=ALU.add), reads=["acc"], writes=["acc"])
                        xbc = xin[0]
                        p.op('scalar', lambda e: e.activation(out=xbc[:, 0:1536], in_=acc[:, 0:1536], func=AF.Silu), reads=["acc"], writes=[("xin", 0)])
                        XB = ("xin", 0)
                        p.dma('sync', dts, PTM[r0:r0 + 128, 6656:6688], writes=["dts"])
                        p.op('vector', lambda e: e.tensor_tensor(out=dts, in0=dts, in1=dtb[:], op=ALU.add), reads=["dts"], writes=["dts"])
                        p.op('scalar', lambda e: e.activation(out=dts, in_=dts, func=AF.Exp), reads=["dts"], writes=["dts"])
                        p.op('scalar', lambda e: e.activation(out=dts, in_=dts, func=AF.Ln, bias=1.0), reads=["dts"], writes=["dts"])
                        p.op('vector', lambda e: e.tensor_tensor(out=la, in0=dts, in1=aneg[:], op=ALU.mult), reads=["dts"], writes=["la"])
                        p.op('vector', lambda e: e.tensor_scalar_mul(out=nla, in0=la, scalar1=-1.0), reads=["la"], writes=["nla"])
                        la_d, nla_d, dt_d = la[:, d * 16:(d + 1) * 16], nla[:, d * 16:(d + 1) * 16], dts[:, d * 16:(d + 1) * 16]
                        xs3 = xbc[:, 0:1024].rearrange("p (h q) -> p h q", h=16)
                        xdt3 = xdt[:, 0:1024].rearrange("p (h q) -> p h q", h=16)
                        xde3 = xde[:, 0:1024].rearrange("p (h q) -> p h q", h=16)
                        p.op('vector', lambda e: e.tensor_tensor(out=xdt3, in0=xs3, in1=bc(dt_d.unsqueeze(2), [128, 16, 64]), op=ALU.mult),
                             reads=[XB, "dts"], writes=["xdt"])
                        p.op('scalar', lambda e: e.copy(out=bcb[:, 0:512], in_=xbc[:, 1024:1536]), reads=[XB], writes=["bcb"])
                        p.mm([lambda e, i=i: e.transpose(psb[:, i * 128:(i + 1) * 128], bcb[:, i * 128:(i + 1) * 128], cstb[:, I_, :]) for i in range(4)],
                             reads=["bcb"], writes=["psb"])
                        p.op('vector', lambda e: e.tensor_copy(out=BCT[:, 0:512], in_=psb[:, 0:512]), reads=["psb"], writes=["BCT"])
                        rhs23 = rhs2[:].rearrange("p (h i) -> p h i", h=16)
                        p.op('vector', lambda e: e.tensor_tensor(out=rhs23, in0=bc(la_d.unsqueeze(2), [128, 16, 128]), in1=bc(Uinc.unsqueeze(1), [128, 16, 128]), op=ALU.mult),
                             reads=["la"], writes=["rhs2"])
                        nl3 = seg[:].rearrange("p (h i) -> p h i", h=16)
                        p.op('vector', lambda e: e.tensor_copy(out=nl3, in_=bc(nla_d.unsqueeze(2), [128, 16, 128])), reads=["nla"], writes=["seg"])
                        for q in range(4):
                            p.mm([lambda e, q=q: e.matmul(ps[q][:], lhsT=cst[:, ONE_, :], rhs=rhs2[:, q * 512:(q + 1) * 512], start=True, stop=False),
                                  lambda e, q=q: e.matmul(ps[q][:], lhsT=Uinc, rhs=seg[:, q * 512:(q + 1) * 512], start=False, stop=False),
                                  lambda e, q=q: e.matmul(ps[q][:].rearrange("p (h i) -> p h i", h=4), lhsT=cst[:, I_, :], rhs=bc(negm.unsqueeze(1), [128, 4, 128]), start=False, stop=True)],
                                 reads=["rhs2", "seg"], writes=[PS(q)])
                        for q in range(4):
                            p.op('scalar', lambda e, q=q: e.activation(out=seg[:, q * 512:(q + 1) * 512], in_=ps[q][:], func=AF.Exp), reads=[PS(q)], writes=["seg"])
                        p.mm([lambda e, g=g: e.matmul(ps[4][:, g * 128:(g + 1) * 128], lhsT=BCT[:, g * 128:(g + 1) * 128], rhs=BCT[:, (2 + g) * 128:(3 + g) * 128], start=True, stop=True)
                              for g in range(2)], reads=["BCT"], writes=[PS(4)])
                        seg4 = seg[:].rearrange("p (g e i) -> p g e i", g=2, e=8)
                        sc4 = sc_[:].rearrange("p (g e i) -> p g e i", g=2, e=8)
                        G4 = bc(ps[4][:, 0:256].rearrange("p (g i) -> p g i", g=2).unsqueeze(2), [128, 2, 8, 128])
                        p.op('vector', lambda e: e.tensor_tensor(out=sc4, in0=seg4, in1=G4, op=ALU.mult), reads=["seg", PS(4)], writes=["sc"])
                        p.mm([lambda e, h=h: e.matmul(ps[h // 8][:, (h % 8) * 64:(h % 8) * 64 + 64], lhsT=sc_[:, h * 128:(h + 1) * 128], rhs=xdt[:, h * 64:(h + 1) * 64], start=True, stop=True)
                              for h in range(16)], reads=["sc", "xdt"], writes=[PS(0), PS(1)])
                        p.mm([lambda e, g=g: e.matmul(ps[2 + g][:], lhsT=BCT[:, (2 + g) * 128:(3 + g) * 128], rhs=hTb[:, g * 512:(g + 1) * 512], start=True, stop=True)
                              for g in range(2)], reads=["BCT", "hTb"], writes=[PS(2), PS(3)])
                        p.mm([lambda e: e.matmul(ps[4][:, 256:272], lhsT=Uinc, rhs=la_d, start=True, stop=True),
                              lambda e: e.matmul(ps[4][:, 272:288], lhsT=Sexc, rhs=la_d, start=True, stop=True),
                              lambda e: e.matmul(ps[4][:, 288:304], lhsT=cst[:, ONE_, :], rhs=la_d, start=True, stop=True)],
                             reads=["la"], writes=[PS(4)])
                        p.op('scalar', lambda e: e.activation(out=ex, in_=ps[4][:, 256:304], func=AF.Exp), reads=[PS(4)], writes=["ex"])
                        ecum, toend, dec = ex[:, 0:16], ex[:, 16:32], ex[:, 32:48]
                        yd3 = ydir[:, 0:1024].rearrange("p (h q) -> p h q", h=16)
                        for g in range(2):
                            p.op('vector', lambda e, g=g: e.tensor_tensor(out=yd3[:, 8 * g:8 * g + 8, :], in0=ps[2 + g][:].rearrange("p (h q) -> p h q", h=8),
                                                                         in1=bc(ecum[:, 8 * g:8 * g + 8].unsqueeze(2), [128, 8, 64]), op=ALU.mult),
                                 reads=[PS(2 + g), "ex"], writes=["ydir"])
                            p.op('vector', lambda e, g=g: e.tensor_tensor(out=ydir[:, g * 512:(g + 1) * 512], in0=ydir[:, g * 512:(g + 1) * 512], in1=ps[g][:], op=ALU.add),
                                 reads=[PS(g), "ydir"], writes=["ydir"])
                        p.op('vector', lambda e: e.tensor_tensor(out=xde3, in0=xdt3, in1=bc(toend.unsqueeze(2), [128, 16, 64]), op=ALU.mult), reads=["xdt", "ex"], writes=["xde"])
                        p.mm([lambda e, g=g: e.matmul(ps[5 + g][:], lhsT=bcb[:, g * 128:(g + 1) * 128], rhs=xde[:, g * 512:(g + 1) * 512], start=True, stop=True) for g in range(2)],
                             reads=["bcb", "xde"], writes=[PS(5), PS(6)])
                        hT3 = hT[:].rearrange("p (h q) -> p h q", h=16)
                        p.op('vector', lambda e: e.tensor_tensor(out=hT3, in0=hT3, in1=bc(dec.unsqueeze(2), [128, 16, 64]), op=ALU.mult), reads=["hT", "ex", "hTb"], writes=["hT"])
                        for g in range(2):
                            p.op('vector', lambda e, g=g: e.tensor_tensor(out=hT[:, g * 512:(g + 1) * 512], in0=hT[:, g * 512:(g + 1) * 512], in1=ps[5 + g][:], op=ALU.add),
                                 reads=[PS(5 + g), "hT"], writes=["hT"])
                        p.op('scalar', lambda e: e.copy(out=hTb[:], in_=hT[:]), reads=["hT"], writes=["hTb"])
                        if d == 0:
                            p.dma('sync', YF[r0:r0 + 128, :], ydir[:, 0:1024], reads=["ydir"], writes=[("YF", b)])
                        else:
                            p.dma('sync', yf[:, 0:1024], YF[r0:r0 + 128, :], reads=[("YF", b)], writes=["yf"])
                            p.dma('sync', zt[:, 0:1024], PTM[r0:r0 + 128, 3072:4096], writes=["zt"])
                            p.op('vector', lambda e: e.tensor_tensor(out=ydir[:, 0:1024], in0=ydir[:, 0:1024], in1=yf[:, 0:1024], op=ALU.add), reads=["ydir", "yf"], writes=["ydir"])
                            yf3 = yf[:, 0:1024].rearrange("p (h q) -> p h q", h=16)
                            p.op('vector', lambda e: e.tensor_tensor(out=yf3, in0=xs3, in1=bc(dsum[:].unsqueeze(2), [128, 16, 64]), op=ALU.mult), reads=[XB, "yf"], writes=["yf"])
                            p.op('vector', lambda e: e.tensor_tensor(out=ydir[:, 0:1024], in0=ydir[:, 0:1024], in1=yf[:, 0:1024], op=ALU.add), reads=["ydir", "yf"], writes=["ydir"])
                            p.op('scalar', lambda e: e.activation(out=zt[:, 0:1024], in_=zt[:, 0:1024], func=AF.Silu), reads=["zt"], writes=["zt"])
                            p.op('vector', lambda e: e.tensor_tensor(out=ydir[:, 0:1024], in0=ydir[:, 0:1024], in1=zt[:, 0:1024], op=ALU.mult), reads=["ydir", "zt"], writes=["ydir"])
                            p.op('scalar', lambda e: e.activation(out=yf[:, 0:1024], in_=ydir[:, 0:1024], func=AF.Square), reads=["ydir"], writes=["yf"])
                            p.op('vector', lambda e: e.tensor_reduce(out=ss1[:, 0:1], in_=yf[:, 0:1024], axis=AX.X, op=ALU.add), reads=["yf"], writes=["ss1"])
                            p.op('vector', lambda e: e.tensor_scalar(out=ss1[:, 0:1], in0=ss1[:, 0:1], scalar1=1.0 / 1024, scalar2=RMS_EPS, op0=ALU.mult, op1=ALU.add), reads=["ss1"], writes=["ss1"])
                            p.op('scalar', lambda e: e.activation(out=ss1[:, 0:1], in_=ss1[:, 0:1], func=AF.Sqrt), reads=["ss1"], writes=["ss1"])
                            p.op('vector', lambda e: e.reciprocal(out=ss1[:, 0:1], in_=ss1[:, 0:1]), reads=["ss1"], writes=["ss1"])
                            p.op('vector', lambda e: e.scalar_tensor_tensor(out=ob16[:, 0:1024], in0=ydir[:, 0:1024], scalar=ss1[:, 0:1], in1=snwt[:], op0=ALU.mult, op1=ALU.mult),
                                 reads=["ydir", "ss1"], writes=["ob16"])
                            emit_T(ob16, 1024, r0, "ob16")
                    if not is_s:
                        p.dma('sync', o_ssd[si, l, d], hT[:], reads=["hT"], writes=[("ossd", si, d)])
                for d_ in range(2):
                    ssd_dir(d_)
            def gla(si, s0, T, is_s):
                p.alias = ALIAS_GLA
                NB = T // 128
                gqT, gkT, gt, eb, enb, od, of_, ggt, junk = big[0], big[1], big[2], big[3], big[4], big[5], big[6], big[7], big[8]
                S = big[9]
                qtb, ktb, khat, gvb, attb, Sb = b16[0], b16[1], b16[2], b16[3], b16[4], b16[5]
                glrT = rp
                ss4 = sm[:, 160:164]
                def gla_dir(d):
                    Uinc = cst[:, U_ if d == 0 else L_, :]
                    Sexc = cst[:, SL_ if d == 0 else SU_, :]
                    last = 127 if d == 0 else 0
                    if is_s:
                        p.dma('sync', S[:, 0:1024], gst[l, d], writes=["S"])
                    else:
                        p.op('vector', lambda e: e.memset(S[:, 0:1024], 0.0), writes=["S"])
                    p.op('scalar', lambda e: e.copy(out=Sb[:, 0:1024], in_=S[:, 0:1024]), reads=["S"], writes=["Sb"])
                    for b in (range(NB) if d == 0 else range(NB - 1, -1, -1)):
                        r0 = s0 + b * 128
                        p.dma('sync', gqT[:, 0:512].rearrange("p (h t) -> p h t", h=4), PFM[0:512, r0:r0 + 128].rearrange("(h p) t -> p h t", p=128), writes=["gqT"])
                        p.dma('sync', gkT[:, 0:512].rearrange("p (h t) -> p h t", h=4), PFM[512:1024, r0:r0 + 128].rearrange("(h p) t -> p h t", p=128), writes=["gkT"])
                        p.dma('sync', glrT[0:16, 0, :].rearrange("p (a t) -> p a t", a=1)[:, 0, :] if False else rp[0:16, :, :].rearrange("p a t -> p (a t)"),
                              PFM[3072 + d * 16:3088 + d * 16, r0:r0 + 128], writes=["glrT"])
                        p.dma('sync', gkT[:, 512:1024], PTM[r0:r0 + 128, 4096:4608], writes=["gktm"])
                        p.dma('sync', junk[:, 0:1024], PTM[r0:r0 + 128, 4608:5632], writes=["gvf"])
                        p.op('scalar', lambda e: e.copy(out=gvb[:, 0:1024], in_=junk[:, 0:1024]), reads=["gvf"], writes=["gvb"])
                        glr2 = rp[0:16, :, :].rearrange("p a t -> p (a t)")
                        p.mm([lambda e: e.matmul(ps[0][:], lhsT=glr2, rhs=ggwt[:, d, :], start=True, stop=True)], reads=["glrT"], writes=[PS(0)])
                        p.op('vector', lambda e: e.tensor_tensor(out=gt[:, 0:512], in0=ps[0][:], in1=ggbt[:, d, :], op=ALU.add), reads=[PS(0)], writes=["gt"])
                        p.op('scalar', lambda e: e.activation(out=gt[:, 0:512], in_=gt[:, 0:512], func=AF.Exp, scale=-1.0), reads=["gt"], writes=["gt"])
                        p.op('scalar', lambda e: e.activation(out=gt[:, 0:512], in_=gt[:, 0:512], func=AF.Ln, bias=1.0), reads=["gt"], writes=["gt"])
                        p.op('vector', lambda e: e.tensor_scalar_mul(out=gt[:, 0:512], in0=gt[:, 0:512], scalar1=-1.0 / 16), reads=["gt"], writes=["gt"])
                        p.mm([lambda e, h=h: e.matmul(ps[1][:, h * 128:(h + 1) * 128], lhsT=gt[:, h * 128:(h + 1) * 128], rhs=Uinc, start=True, stop=True) for h in range(4)],
                             reads=["gt"], writes=[PS(1)])
                        p.mm([lambda e: e.matmul(ps[2][:], lhsT=Sexc, rhs=gt[:, 0:512], start=True, stop=True)], reads=["gt"], writes=[PS(2)])
                        p.op('scalar', lambda e: e.activation(out=eb[:, 0:512], in_=ps[1][:], func=AF.Exp), reads=[PS(1)], writes=["eb"])
                        p.op('scalar', lambda e: e.activation(out=enb[:, 0:512], in_=ps[1][:], func=AF.Exp, scale=-1.0), reads=[PS(1)], writes=["enb"])
                        p.op('scalar', lambda e: e.activation(out=enb[:, 512:1024], in_=ps[2][:], func=AF.Exp), reads=[PS(2)], writes=["et"])
                        p.op('vector', lambda e: e.scalar_tensor_tensor(out=qtb[:, 0:512], in0=gqT[:, 0:512], scalar=SCALE, in1=eb[:, 0:512], op0=ALU.mult, op1=ALU.mult),
                             reads=["gqT", "eb"], writes=["qtb"])
                        p.op('vector', lambda e: e.tensor_tensor(out=ktb[:, 0:512], in0=gkT[:, 0:512], in1=enb[:, 0:512], op=ALU.mult), reads=["gkT", "enb"], writes=["ktb"])
                        p.op('vector', lambda e: e.tensor_tensor(out=khat[:, 0:512], in0=gkT[:, 512:1024], in1=enb[:, 512:1024], op=ALU.mult), reads=["gktm", "et"], writes=["khat"])
                        p.mm([lambda e, h=h: e.matmul(ps[0][:, h * 128:(h + 1) * 128], lhsT=ktb[:, h * 128:(h + 1) * 128], rhs=qtb[:, h * 128:(h + 1) * 128], start=True, stop=True)
                              for h in range(4)], reads=["ktb", "qtb"], writes=[PS(0)])
                        p.op('vector', lambda e: e.tensor_tensor(out=attb[:, 0:512].rearrange("p (h i) -> p h i", h=4), in0=ps[0][:].rearrange("p (h i) -> p h i", h=4),
                                                                 in1=bc(Uinc.unsqueeze(1), [128, 4, 128]), op=ALU.mult), reads=[PS(0)], writes=["attb"])
                        fns = []
                        for h in range(4):
                            o_ap = ps[3 + h // 2][:, (h % 2) * 256:(h % 2) * 256 + 256]
                            fns.append(lambda e, h=h, o_ap=o_ap: e.matmul(o_ap, lhsT=attb[:, h * 128:(h + 1) * 128], rhs=gvb[:, h * 256:(h + 1) * 256], start=True, stop=False))
                            fns.append(lambda e, h=h, o_ap=o_ap: e.matmul(o_ap, lhsT=qtb[:, h * 128:(h + 1) * 128], rhs=Sb[:, h * 256:(h + 1) * 256], start=False, stop=True))
                        p.mm(fns, reads=["attb", "gvb", "qtb", "Sb"], writes=[PS(3), PS(4)])
                        p.mm([lambda e, h=h: e.matmul(ps[5 + h // 2][:, (h % 2) * 256:(h % 2) * 256 + 256], lhsT=khat[:, h * 128:(h + 1) * 128], rhs=gvb[:, h * 256:(h + 1) * 256], start=True, stop=True)
                              for h in range(4)], reads=["khat", "gvb"], writes=[PS(5), PS(6)])
                        for h in range(4):
                            p.op('vector', lambda e, h=h: e.scalar_tensor_tensor(out=S[:, h * 256:(h + 1) * 256], in0=S[:, h * 256:(h + 1) * 256],
                                                                                scalar=eb[:, h * 128 + last:h * 128 + last + 1],
                                                                                in1=ps[5 + h // 2][:, (h % 2) * 256:(h % 2) * 256 + 256], op0=ALU.mult, op1=ALU.add),
                                 reads=["S", "eb", PS(5 + h // 2), "Sb"], writes=["S"])
                        p.op('scalar', lambda e: e.copy(out=Sb[:, 0:1024], in_=S[:, 0:1024]), reads=["S"], writes=["Sb"])
                        if d == 0:
                            for hh in range(2):
                                p.op('vector', lambda e, hh=hh: e.tensor_copy(out=od[:, hh * 512:(hh + 1) * 512], in_=ps[3 + hh][:]), reads=[PS(3 + hh)], writes=["od"])
                            p.dma('sync', OF[r0:r0 + 128, :], od[:, 0:1024], reads=["od"], writes=[("OF", b)])
                        else:
                            p.dma('sync', of_[:, 0:1024], OF[r0:r0 + 128, :], reads=[("OF", b)], writes=["of_"])
                            p.dma('sync', ggt[:, 0:1024], PTM[r0:r0 + 128, 5632:6656], writes=["ggt"])
                            for hh in range(2):
                                p.op('vector', lambda e, hh=hh: e.tensor_tensor(out=od[:, hh * 512:(hh + 1) * 512], in0=ps[3 + hh][:], in1=of_[:, hh * 512:(hh + 1) * 512], op=ALU.add),
                                     reads=[PS(3 + hh), "of_"], writes=["od"])
                            p.op('scalar', lambda e: e.activation(out=of_[:, 0:1024], in_=od[:, 0:1024], func=AF.Square), reads=["od"], writes=["of_"])
                            p.op('vector', lambda e: e.tensor_reduce(out=ss4, in_=of_[:, 0:1024].rearrange("p (h v) -> p h v", h=4), axis=AX.X, op=ALU.add), reads=["of_"], writes=["ss4"])
                            p.op('vector', lambda e: e.tensor_scalar(out=ss4, in0=ss4, scalar1=1.0 / 256, scalar2=RMS_EPS, op0=ALU.mult, op1=ALU.add), reads=["ss4"], writes=["ss4"])
                            p.op('scalar', lambda e: e.activation(out=ss4, in_=ss4, func=AF.Sqrt), reads=["ss4"], writes=["ss4"])
                            p.op('vector', lambda e: e.reciprocal(out=ss4, in_=ss4), reads=["ss4"], writes=["ss4"])
                            od3 = od[:, 0:1024].rearrange("p (h v) -> p h v", h=4)
                            p.op('vector', lambda e: e.tensor_tensor(out=od3, in0=od3, in1=bc(ss4.unsqueeze(2), [128, 4, 256]), op=ALU.mult), reads=["od", "ss4"], writes=["od"])
                            p.op('vector', lambda e: e.tensor_tensor(out=od[:, 0:1024], in0=od[:, 0:1024], in1=gnwt[:], op=ALU.mult), reads=["od"], writes=["od"])
                            p.op('scalar', lambda e: e.activation(out=ggt[:, 0:1024], in_=ggt[:, 0:1024], func=AF.Silu), reads=["ggt"], writes=["ggt"])
                            p.op('vector', lambda e: e.tensor_tensor(out=attb[:, 0:1024], in0=od[:, 0:1024], in1=ggt[:, 0:1024], op=ALU.mult), reads=["od", "ggt"], writes=["attb"])
                            emit_T(attb, 2048, r0, "attb")
                    if not is_s:
                        p.dma('sync', o_gla[si, l, d], S[:, 0:1024], reads=["S"], writes=[("ogla", si, d)])
                for d_ in range(2):
                    gla_dir(d_)

            def lru(si, s0, T, is_s):
                p.alias = ALIAS_LRU
                xl, rt, it_, at, ut, hf, hb_, lgT, tmp, xp = big
                xlb, ob = b16[0], b16[1]
                wab, wxb = b16[2][:, 0:128], b16[2][:, 128:256]
                for n in range(8):
                    p.op('vector', lambda e: e.memset(xp[:, 0:2], 0.0), writes=["xp"])
                    p.op('vector', lambda e: e.memset(xp[:, T + 2:T + 3], 0.0), writes=["xp"])
                    p.dma('sync', xp[:, 2:T + 2], PFM[1024 + n * 128:1152 + n * 128, s0:s0 + T], writes=["xp"])
                    p.dma('sync', lgT[:, 0:T], PFM[2048 + n * 128:2176 + n * 128, s0:s0 + T], writes=["lgT"])
                    p.op('vector', lambda e, n=n: e.tensor_scalar_mul(out=xl[:, 0:T], in0=xp[:, 0:T], scalar1=lcwt[:, n, 0:1]), reads=["xp"], writes=["xl"])
                    for j in range(1, 4):
                        p.op('vector', lambda e, n=n, j=j: e.scalar_tensor_tensor(out=xl[:, 0:T], in0=xp[:, j:j + T], scalar=lcwt[:, n, j:j + 1], in1=xl[:, 0:T], op0=ALU.mult, op1=ALU.add),
                             reads=["xp", "xl"], writes=["xl"])
                    p.op('vector', lambda e, n=n: e.tensor_scalar_add(out=xl[:, 0:T], in0=xl[:, 0:T], scalar1=lcbt[:, n:n + 1]), reads=["xl"], writes=["xl"])
                    p.op('scalar', lambda e: e.copy(out=xlb[:, 0:T], in_=xl[:, 0:T]), reads=["xl"], writes=["xlb"])
                    for d in range(2):
                        p.dma('gpsimd', wab, lwa[l, d, n], writes=["wab"])
                        p.dma('gpsimd', wxb, lwx[l, d, n], writes=["wxb"])
                        for tcq in range(T // 512 if T >= 512 else 1):
                            w_ = min(512, T)
                            cs = slice(tcq * 512, tcq * 512 + w_)
                            p.mm([lambda e, cs=cs, w_=w_: e.matmul(ps[0][:, 0:w_], lhsT=wab, rhs=xlb[:, cs], start=True, stop=True)], reads=["wab", "xlb"], writes=[PS(0)])
                            p.mm([lambda e, cs=cs, w_=w_: e.matmul(ps[1][:, 0:w_], lhsT=wxb, rhs=xlb[:, cs], start=True, stop=True)], reads=["wxb", "xlb"], writes=[PS(1)])
                            p.op('scalar', lambda e, cs=cs, w_=w_, d=d, n=n: e.activation(out=rt[:, cs], in_=ps[0][:, 0:w_], func=AF.Sigmoid, bias=lbat[:, d, n:n + 1]), reads=[PS(0)], writes=["rt"])
                            p.op('scalar', lambda e, cs=cs, w_=w_, d=d, n=n: e.activation(out=it_[:, cs], in_=ps[1][:, 0:w_], func=AF.Sigmoid, bias=lbxt[:, d, n:n + 1]), reads=[PS(1)], writes=["it"])
                        p.op('scalar', lambda e, d=d, n=n: e.activation(out=at[:, 0:T], in_=rt[:, 0:T], func=AF.Exp, scale=c8[:, d, n:n + 1]), reads=["rt"], writes=["at"])
                        p.op('scalar', lambda e, d=d, n=n: e.activation(out=ut[:, 0:T], in_=rt[:, 0:T], func=AF.Exp, scale=c16[:, d, n:n + 1]), reads=["rt"], writes=["ut"])
                        p.op('scalar', lambda e: e.activation(out=ut[:, 0:T], in_=ut[:, 0:T], func=AF.Sqrt, scale=-1.0, bias=1.0), reads=["ut"], writes=["ut"])
                        p.op('vector', lambda e: e.tensor_tensor(out=ut[:, 0:T], in0=ut[:, 0:T], in1=it_[:, 0:T], op=ALU.mult), reads=["ut", "it"], writes=["ut"])
                        p.op('vector', lambda e: e.tensor_tensor(out=ut[:, 0:T], in0=ut[:, 0:T], in1=xl[:, 0:T], op=ALU.mult), reads=["ut", "xl"], writes=["ut"])
                        init = lstt[:, d, n:n + 1] if is_s else 0.0
                        hd = hf if d == 0 else hb_
                        if d == 0:
                            p.op('vector', lambda e, init=init: e.tensor_tensor_scan(out=hf[:, 0:T], data0=at[:, 0:T], data1=ut[:, 0:T], initial=init, op0=ALU.mult, op1=ALU.add),
                                 reads=["at", "ut"], writes=["hf"])
                        else:
                            p.op('vector', lambda e, init=init: e.tensor_tensor_scan(out=hb_[:, T - 1::-1] if False else hb_[:, 0:T][:, ::-1], data0=at[:, 0:T][:, ::-1], data1=ut[:, 0:T][:, ::-1],
                                                                                    initial=init, op0=ALU.mult, op1=ALU.add), reads=["at", "ut"], writes=["hb"])
                        if not is_s:
                            col = T - 1 if d == 0 else 0
                            p.op('vector', lambda e, hd=hd, col=col, d=d, n=n: e.tensor_copy(out=lruo[:, si, l, d, n:n + 1], in_=hd[:, col:col + 1]),
                                 reads=["hf", "hb"], writes=["lruo"])
                    p.op('vector', lambda e: e.tensor_tensor(out=hf[:, 0:T], in0=hf[:, 0:T], in1=hb_[:, 0:T], op=ALU.add), reads=["hf", "hb"], writes=["hf"])
                    p.op('scalar', lambda e: e.activation(out=tmp[:, 0:T], in_=lgT[:, 0:T], func=AF.Square), reads=["lgT"], writes=["tmp"])
                    p.op('vector', lambda e: e.tensor_scalar(out=tmp[:, 0:T], in0=tmp[:, 0:T], scalar1=0.044715, scalar2=1.0, op0=ALU.mult, op1=ALU.add), reads=["tmp"], writes=["tmp"])
                    p.op('vector', lambda e: e.tensor_tensor(out=tmp[:, 0:T], in0=tmp[:, 0:T], in1=lgT[:, 0:T], op=ALU.mult), reads=["tmp", "lgT"], writes=["tmp"])
                    p.op('scalar', lambda e: e.activation(out=tmp[:, 0:T], in_=tmp[:, 0:T], func=AF.Tanh, scale=0.7978845608028654), reads=["tmp"], writes=["tmp"])
                    p.op('vector', lambda e: e.tensor_scalar(out=tmp[:, 0:T], in0=tmp[:, 0:T], scalar1=0.5, scalar2=0.5, op0=ALU.mult, op1=ALU.add), reads=["tmp"], writes=["tmp"])
                    p.op('vector', lambda e: e.tensor_tensor(out=tmp[:, 0:T], in0=tmp[:, 0:T], in1=lgT[:, 0:T], op=ALU.mult), reads=["tmp", "lgT"], writes=["tmp"])
                    p.op('vector', lambda e: e.tensor_tensor(out=ob[:, 0:T], in0=hf[:, 0:T], in1=tmp[:, 0:T], op=ALU.mult), reads=["hf", "tmp"], writes=["ob"])
                    p.dma('sync', CTs[3072 + n * 128:3200 + n * 128, s0:s0 + T], ob[:, 0:T], reads=["ob"], writes=[("CTl", n, s0)])

            for si, (s0, T, is_s) in enumerate(cfg.SEQS):
                attention(si, s0, T, is_s)
                ssd(si, s0, T, is_s)
                gla(si, s0, T, is_s)
                lru(si, s0, T, is_s)
            p.alias = {}
            p.barrier()

    tile_stage(0)
    mixer_stage(0)
    if STOP_AFTER != "mix0":
        tile_stage(1)
        mixer_stage(1)
        tile_stage(2)
    if STOP_AFTER is None:
        p.dma('sync', o_lru[0], lruo[:, 0], reads=["lruo"], writes=["olru0"])
        p.dma('sync', o_lru[1], lruo[:, 1], reads=["lruo"], writes=["olru1"])
    p.barrier()
    with nc.Block() as block:
        p.emit(block)
    return B


IN_SPLITS = (1024, 256, 256, 1024, 1024, 256, 256, 32, 512, 512, 1024, 32, 1024, 1024, 1024)
IN_NAMES = ('aq', 'ak', 'av', 'sx', 'sz', 'sb', 'sc', 'sdt', 'gq', 'gk', 'gv', 'glr', 'gg', 'lx', 'lg')
IN_OFF = {}
_o = 0
for _n, _s in zip(IN_NAMES, IN_SPLITS):
    IN_OFF[_n] = (_o, _o + _s)
    _o += _s


def _bcast(v):
    return np.ascontiguousarray(np.broadcast_to(v[None], (128,) + v.shape)).astype(np.float32)


def _rope_tables(T):
    rows = T // 64
    row = np.repeat(np.arange(rows, dtype=np.float32), 64)
    col = np.tile(np.arange(64, dtype=np.float32), rows)
    inv = (np.float32(10000.0) ** (-np.arange(32, dtype=np.float32) / np.float32(32))).astype(np.float32)
    ang = np.stack([row[:, None] * inv, col[:, None] * inv], axis=1).astype(np.float32)
    return np.stack([np.cos(ang).reshape(T, 64), np.sin(ang).reshape(T, 64)], axis=1).astype(np.float32)


def _consts():
    k = np.arange(128)[:, None]
    i = np.arange(128)[None, :]
    c = np.zeros((128, 8, 128), np.float32)
    c[:, 0] = (k <= i)
    c[:, 1] = (k >= i)
    c[:, 2] = (k < i)
    c[:, 3] = (k > i)
    c[:, 4] = (k == i)
    c[:, 5] = 1.0
    c[:, 6] = np.where(k > i, -30000.0, 0.0)
    c[:, 7] = np.where(k < i, -30000.0, 0.0)
    return c


def prep_shared(inp, cfg):
    f32 = np.float32
    sh = {}
    FCn = cfg.FC
    for l in range(2):
        mw = tile_w_stationary(np.asarray(inp['mod_w'][l], f32)).reshape(288, 128, KC * 128)
        sh[f'modw{l}0'] = np.ascontiguousarray(mw[:144])
        sh[f'modw{l}1'] = np.ascontiguousarray(mw[144:])
        del mw
        for f in range(2):
            sh[f'wg{l}{f}'] = tile_w_stationary(np.asarray(inp['ffn_w_gate'][l, f], f32)).reshape(FCn, 128, KC * 128)
            sh[f'wu{l}{f}'] = tile_w_stationary(np.asarray(inp['ffn_w_up'][l, f], f32)).reshape(FCn, 128, KC * 128)
            wdn = np.asarray(inp['ffn_w_down'][l, f], f32)
            sh[f'wd{l}{f}'] = np.ascontiguousarray(wdn.reshape(FCn, 128, 16, 256).transpose(2, 1, 0, 3)).reshape(16, 128, FCn * 256)
        w = np.asarray(inp['w_in'][l], f32)
        cols = lambda names: np.concatenate([w[:, IN_OFF[n][0]:IN_OFF[n][1]] for n in names], axis=1)
        wt = cols(['aq', 'ak', 'av', 'sx', 'sb', 'sc', 'sz', 'gk', 'gv', 'gg', 'sdt'])
        wt = np.concatenate([wt, np.zeros((D, TM_COLS - wt.shape[1]), f32)], axis=1)
        sh[f'wtm{l}'] = tile_w_stationary(wt, 256).reshape(27, 128, KC * 256)
        wf = cols(['gq', 'gk', 'lx', 'lg', 'glr'])
        wf = np.concatenate([wf, np.zeros((D, FM_ROWS - wf.shape[1]), f32)], axis=1)
        sh[f'wfm{l}'] = tile_w_stationary(wf, 128).reshape(25, 128, KC * 128)
        sh[f'wout{l}'] = tile_w_stationary(np.asarray(inp['w_out'][l], f32)).reshape(32, 128, KC * 128)
    sh['modb'] = feat_major_vec(np.asarray(inp['mod_b'], f32))
    sh['lng'] = feat_major_vec(np.asarray(inp['ln_g'], f32))
    sh['lnb'] = feat_major_vec(np.asarray(inp['ln_b'], f32))
    qn, kn = np.asarray(inp['q_norm'], f32), np.asarray(inp['k_norm'], f32)
    sh['qkn'] = _bcast(np.concatenate([np.tile(qn, (1, 8)), np.tile(kn, (1, 2))], axis=1))
    sh['rope'] = _rope_tables(cfg.TS)
    sh['sconvw'] = _bcast(np.asarray(inp['ssd_conv_w'], f32))
    sh['sconvb'] = _bcast(np.asarray(inp['ssd_conv_b'], f32))
    sh['salog'] = _bcast(np.asarray(inp['ssd_a_log'], f32).reshape(2, 32))
    sh['sdtb'] = _bcast(np.asarray(inp['ssd_dt_bias'], f32).reshape(2, 32))
    sh['sdd'] = _bcast(np.asarray(inp['ssd_d'], f32))
    sh['snw'] = _bcast(np.asarray(inp['ssd_norm_w'], f32))
    sh['ggw'] = np.ascontiguousarray(np.asarray(inp['gla_gate_w'], f32).transpose(2, 0, 1, 3))
    sh['ggb'] = _bcast(np.asarray(inp['gla_gate_b'], f32))
    sh['gnw'] = _bcast(np.tile(np.asarray(inp['gla_norm_w'], f32), (1, 4)))
    sh['lcw'] = np.ascontiguousarray(np.asarray(inp['lru_conv_w'], f32).reshape(2, 4, 8, 128).transpose(3, 0, 2, 1))
    sh['lcb'] = np.ascontiguousarray(np.asarray(inp['lru_conv_b'], f32).reshape(2, 8, 128).transpose(2, 0, 1))
    sh['lwa'] = np.ascontiguousarray(np.asarray(inp['lru_wa'], f32))
    sh['lwx'] = np.ascontiguousarray(np.asarray(inp['lru_wx'], f32))
    for k_, n_ in (('lba', 'lru_ba'), ('lbx', 'lru_bx'), ('llam', 'lru_lam')):
        sh[k_] = np.ascontiguousarray(np.asarray(inp[n_], f32).reshape(2, 2, 8, 128).transpose(3, 0, 1, 2))
    sh['consts'] = _consts()
    return sh


def prep_core(inp, c):
    f32 = np.float32
    b = c // 2
    m = {}
    xp, xs = np.asarray(inp['x_prompt'], f32), np.asarray(inp['x_sample'], f32)
    m['xT'] = np.ascontiguousarray(np.concatenate([xp[2 * c].T, xp[2 * c + 1].T, xs[b].T], axis=1))
    cond = np.stack([np.asarray(inp['c_ctx'], f32), np.asarray(inp['c'], f32)[b]], axis=0)
    m['condT'] = np.ascontiguousarray(feat_major_vec(cond).transpose(0, 2, 1))
    m['ck'] = np.ascontiguousarray(np.asarray(inp['cache_attn_k'], f32)[b].reshape(2, 256, 256))
    m['cv'] = np.ascontiguousarray(np.asarray(inp['cache_attn_v'], f32)[b].reshape(2, 256, 256))
    m['sst'] = np.ascontiguousarray(np.asarray(inp['state_ssd'], f32)[b].transpose(0, 1, 4, 2, 3).reshape(2, 2, 128, 1024))
    m['gst'] = np.ascontiguousarray(np.asarray(inp['state_gla'], f32)[b].transpose(0, 1, 3, 2, 4).reshape(2, 2, 128, 1024))
    m['lst'] = np.ascontiguousarray(np.asarray(inp['state_lru'], f32)[b].reshape(2, 2, 8, 128).transpose(3, 0, 1, 2))
    return m


def run(inp, cfg, ncores):
    B = build_program(cfg)
    sh = prep_shared(inp, cfg)
    in_maps = []
    for c in range(ncores):
        m = dict(sh)
        m.update(prep_core(inp, c))
        in_maps.append(m)
    res = run_bass_kernel_spmd(B.nc, in_maps, core_ids=list(range(ncores)))
    return assemble(res.results, cfg, ncores)


def assemble(R, cfg, ncores):
    nb, ns = 2 * ncores, (ncores + 1) // 2
    TS = cfg.TS
    f32 = np.float32
    y_prompt = np.zeros((nb, 256, D), f32)
    y_sample = np.zeros((ns, TS, D), f32)
    nk = np.zeros((nb, 2, 256, 2, 128), f32)
    nv = np.zeros((nb, 2, 256, 2, 128), f32)
    nssd = np.zeros((nb, 2, 2, 16, 64, 128), f32)
    ngla = np.zeros((nb, 2, 2, 4, 128, 256), f32)
    nlru = np.zeros((nb, 2, 2, 1024), f32)
    for c in range(ncores):
        r = R[c]
        yT = r['yT']
        for s in range(2):
            y_prompt[2 * c + s] = yT[:, s * 256:(s + 1) * 256].T
            nk[2 * c + s] = r['o_k'][s].reshape(2, 256, 2, 128)
            nv[2 * c + s] = r['o_v'][s].reshape(2, 256, 2, 128)
            nssd[2 * c + s] = r['o_ssd'][s].reshape(2, 2, 128, 16, 64).transpose(0, 1, 3, 4, 2)
            ngla[2 * c + s] = r['o_gla'][s].reshape(2, 2, 128, 4, 256).transpose(0, 1, 3, 2, 4)
            nlru[2 * c + s] = r['o_lru'][s].transpose(1, 2, 3, 0).reshape(2, 2, 1024)
        if c % 2 == 0:
            y_sample[c // 2] = yT[:, 512:].T
    return (y_prompt, y_sample, nk, nv, nssd, ngla, nlru)


def kernel(**inputs):
    return run(inputs, Cfg(2048, FC), NCORES)
```
